# Optimizing a Trainium2 kernel written in Bass

```python
import math
import jax, jax.numpy as jnp
from jax import lax
import numpy as np

D_MODEL = 2048
BATCH = 4
SEQ = 2048
DEPTH = 4

RG_WIDTH = D_MODEL // 2
RG_BLOCKS = 16
RG_BLOCK_DIM = RG_WIDTH // RG_BLOCKS
RG_CONV = 4
RG_C = 8.0
ATTN_HEADS = 8
ATTN_HEAD_DIM = 64
ATTN_V_DIM = 2 * ATTN_HEAD_DIM
ATTN_QK_WIDTH = ATTN_HEADS * 2 * ATTN_HEAD_DIM
ATTN_WIDTH = ATTN_HEADS * ATTN_V_DIM
MIX_WIDTH = RG_WIDTH + ATTN_WIDTH
IN_WIDTH = 2 * RG_WIDTH + 2 * ATTN_QK_WIDTH + ATTN_WIDTH
IN_SPLITS = (RG_WIDTH, 2 * RG_WIDTH, 2 * RG_WIDTH + ATTN_QK_WIDTH, 2 * RG_WIDTH + 2 * ATTN_QK_WIDTH)
D_FF = 5632
FFN_CONV = 3
Q_BLOCK = 128
DEEPNORM_ALPHA = (2.0 * DEPTH) ** 0.25
DEEPNORM_BETA = (8.0 * DEPTH) ** -0.25
LN_EPS = 1e-5
RMS_EPS = 1e-5

kernel_name = "hymba_rglru_diffattn_alibi_convffn_deepnorm"


def _layernorm(x, g, b):
    xf = x.astype(jnp.float32)
    mu = jnp.mean(xf, axis=-1, keepdims=True)
    var = jnp.mean(jnp.square(xf - mu), axis=-1, keepdims=True)
    y = (xf - mu) * lax.rsqrt(var + LN_EPS) * g.astype(jnp.float32) + b.astype(jnp.float32)
    return y.astype(x.dtype)


def _causal_dwconv(x, w, b):
    K = w.shape[0]
    S = x.shape[1]
    xp = jnp.pad(x, ((0, 0), (K - 1, 0), (0, 0)))
    out = xp[:, 0:S] * w[0]
    for k in range(1, K):
        out = out + xp[:, k:k + S] * w[k]
    return out + b


def _lru_combine(left, right):
    a_l, b_l = left
    a_r, b_r = right
    return a_l * a_r, a_r * b_l + b_r


def _rglru_group(rg_x, rg_gate, conv_w, conv_b, wa, ba, wx, bx, lam):
    B, S, _ = rg_x.shape
    u = _causal_dwconv(rg_x, conv_w, conv_b)
    ub = u.reshape(B, S, RG_BLOCKS, RG_BLOCK_DIM)
    r = jax.nn.sigmoid(jnp.einsum('bsgi,gij->bsgj', ub, wa).reshape(B, S, RG_WIDTH) + ba)
    ig = jax.nn.sigmoid(jnp.einsum('bsgi,gij->bsgj', ub, wx).reshape(B, S, RG_WIDTH) + bx)
    log_a = RG_C * r.astype(jnp.float32) * jax.nn.log_sigmoid(lam.astype(jnp.float32))
    a = jnp.exp(log_a)
    mult = jnp.sqrt(-jnp.expm1(2.0 * log_a))
    bterm = mult * (ig * u).astype(jnp.float32)
    _, h = lax.associative_scan(_lru_combine, (a, bterm), axis=1)
    return h.astype(rg_x.dtype) * jax.nn.gelu(rg_gate)


def _diff_attn_group(q, k, v, layer_idx, lq1, lk1, lq2, lk2, subln_g):
    B, S, _ = q.shape
    H, d, V = ATTN_HEADS, ATTN_HEAD_DIM, ATTN_V_DIM
    kh = k.reshape(B, S, H, 2, d)
    vh = v.reshape(B, S, H, V)
    lam_init = 0.8 - 0.6 * math.exp(-0.3 * layer_idx)
    lam = (jnp.exp(jnp.sum(lq1.astype(jnp.float32) * lk1.astype(jnp.float32)))
           - jnp.exp(jnp.sum(lq2.astype(jnp.float32) * lk2.astype(jnp.float32))) + lam_init)
    slopes = jnp.exp2(-8.0 * jnp.arange(1, H + 1, dtype=jnp.float32) / H)
    scale = d ** -0.5
    nb = S // Q_BLOCK
    qb = q.reshape(B, nb, Q_BLOCK, H, 2, d).transpose(1, 0, 2, 3, 4, 5)
    starts = jnp.arange(nb, dtype=jnp.int32) * Q_BLOCK
    kpos = jnp.arange(S, dtype=jnp.int32)
    neg = jnp.finfo(jnp.float32).min

    def block(args):
        q_blk, start = args
        qpos = start + jnp.arange(Q_BLOCK, dtype=jnp.int32)
        dist = qpos[:, None] - kpos[None, :]
        bias = -slopes[:, None, None] * dist.astype(jnp.float32)[None]
        bias = jnp.where((dist >= 0)[None], bias, neg)
        s = jnp.einsum('bqhcd,bkhcd->bhcqk', q_blk, kh,
                       preferred_element_type=jnp.float32) * scale + bias[None, :, None]
        p = jax.nn.softmax(s, axis=-1)
        wgt = p[:, :, 0] - lam * p[:, :, 1]
        return jnp.einsum('bhqk,bkhv->bqhv', wgt.astype(vh.dtype), vh)

    o = lax.map(block, (qb, starts))
    o = o.transpose(1, 0, 2, 3, 4).reshape(B, S, H, V)
    of = o.astype(jnp.float32)
    of = of * lax.rsqrt(jnp.mean(jnp.square(of), axis=-1, keepdims=True) + RMS_EPS)
    of = of * subln_g.astype(jnp.float32) * (1.0 - lam_init)
    return of.astype(q.dtype).reshape(B, S, ATTN_WIDTH)


def setup_inputs(seed: int = 0) -> dict:
    key = jax.random.key(seed)
    ks = jax.random.split(key, 24)
    f32 = jnp.float32
    nrm = lambda k, shape, s: jax.random.normal(k, shape, f32) * s
    a0 = jax.random.uniform(ks[9], (DEPTH, RG_WIDTH), f32, minval=0.9, maxval=0.999)
    return {
        "x": jax.random.normal(ks[0], (BATCH, SEQ, D_MODEL), f32),
        "w_in": nrm(ks[1], (DEPTH, D_MODEL, IN_WIDTH), D_MODEL ** -0.5),
        "rg_conv_w": nrm(ks[2], (DEPTH, RG_CONV, RG_WIDTH), RG_CONV ** -0.5),
        "rg_conv_b": nrm(ks[3], (DEPTH, RG_WIDTH), 0.01),
        "rg_gate_a_w": nrm(ks[4], (DEPTH, RG_BLOCKS, RG_BLOCK_DIM, RG_BLOCK_DIM), RG_BLOCK_DIM ** -0.5),
        "rg_gate_a_b": nrm(ks[5], (DEPTH, RG_WIDTH), 0.01),
        "rg_gate_x_w": nrm(ks[6], (DEPTH, RG_BLOCKS, RG_BLOCK_DIM, RG_BLOCK_DIM), RG_BLOCK_DIM ** -0.5),
        "rg_gate_x_b": nrm(ks[7], (DEPTH, RG_WIDTH), 0.01),
        "rg_lambda": jnp.log(a0) - jnp.log1p(-a0),
        "lam_q1": nrm(ks[10], (DEPTH, ATTN_HEAD_DIM), 0.1),
        "lam_k1": nrm(ks[11], (DEPTH, ATTN_HEAD_DIM), 0.1),
        "lam_q2": nrm(ks[12], (DEPTH, ATTN_HEAD_DIM), 0.1),
        "lam_k2": nrm(ks[13], (DEPTH, ATTN_HEAD_DIM), 0.1),
        "subln_g": 1.0 + nrm(ks[14], (DEPTH, ATTN_V_DIM), 0.02),
        "w_out": nrm(ks[15], (DEPTH, MIX_WIDTH, D_MODEL), MIX_WIDTH ** -0.5 * DEEPNORM_BETA),
        "ln_mix_g": 1.0 + nrm(ks[16], (DEPTH, D_MODEL), 0.02),
        "ln_mix_b": nrm(ks[17], (DEPTH, D_MODEL), 0.02),
        "w_up": nrm(ks[18], (DEPTH, D_MODEL, 2 * D_FF), D_MODEL ** -0.5),
        "ffn_conv_w": nrm(ks[19], (DEPTH, FFN_CONV, 2 * D_FF), FFN_CONV ** -0.5),
        "ffn_conv_b": nrm(ks[20], (DEPTH, 2 * D_FF), 0.01),
        "w_down": nrm(ks[21], (DEPTH, D_FF, D_MODEL), D_FF ** -0.5 * DEEPNORM_BETA),
        "ln_ffn_g": 1.0 + nrm(ks[22], (DEPTH, D_MODEL), 0.02),
        "ln_ffn_b": nrm(ks[23], (DEPTH, D_MODEL), 0.02),
    }


def reference(x, w_in, rg_conv_w, rg_conv_b, rg_gate_a_w, rg_gate_a_b, rg_gate_x_w, rg_gate_x_b,
              rg_lambda, lam_q1, lam_k1, lam_q2, lam_k2, subln_g, w_out, ln_mix_g, ln_mix_b,
              w_up, ffn_conv_w, ffn_conv_b, w_down, ln_ffn_g, ln_ffn_b):
    for i in range(DEPTH):
        proj = jnp.einsum('bsd,de->bse', x, w_in[i])
        rg_x, rg_gate, q, k, v = jnp.split(proj, IN_SPLITS, axis=-1)
        rg_out = _rglru_group(rg_x, rg_gate, rg_conv_w[i], rg_conv_b[i], rg_gate_a_w[i],
                              rg_gate_a_b[i], rg_gate_x_w[i], rg_gate_x_b[i], rg_lambda[i])
        at_out = _diff_attn_group(q, k, v, i, lam_q1[i], lam_k1[i], lam_q2[i], lam_k2[i], subln_g[i])
        mix = jnp.einsum('bsm,md->bsd', jnp.concatenate([rg_out, at_out], axis=-1), w_out[i])
        x = _layernorm(DEEPNORM_ALPHA * x + mix, ln_mix_g[i], ln_mix_b[i])
        hu = _causal_dwconv(jnp.einsum('bsd,df->bsf', x, w_up[i]), ffn_conv_w[i], ffn_conv_b[i])
        g, u = jnp.split(hu, 2, axis=-1)
        ffn = jnp.einsum('bsf,fd->bsd', jax.nn.gelu(g) * u, w_down[i])
        x = _layernorm(DEEPNORM_ALPHA * x + ffn, ln_ffn_g[i], ln_ffn_b[i])
    return x
```

```python
import contextlib
import math
import numpy as np
import concourse.bass as bass
import concourse.mybir as mybir
from concourse.bass_utils import run_bass_kernel_spmd

F32 = mybir.dt.float32
BF16 = mybir.dt.bfloat16
AF = mybir.ActivationFunctionType
ALU = mybir.AluOpType
AX = mybir.AxisListType

D = 2048
S = 2048
DEPTH = 4
DFF = 5632
INW = 5120
TT = 512
NT = S // TT
ALPHA = (2.0 * DEPTH) ** 0.25
NEG = -1.0e6
ENGS = ("pe", "act", "dve", "pool", "sp")

V_CW = 0
V_CB = 32
V_BA = 40
V_BX = 48
V_LAM = 56
V_G1 = 64
V_B1 = 80
V_G2 = 96
V_B2 = 112
V_FCW = 128
V_FCB = 392
V_SUBG = 480
V_LAMV = 481
V_NLI = 737
V_SUBK = 738
V_SUBE = 739
V_LNEPS = 740
V_ONE = 741
NVL = 742
DR_HBA = 0
DR_HBX = 8
DR_CLH = 16
DR_CL = 24
DR_T0 = 32
DR_NEGLAM = 48
DR_S = 50
DR_P = 64
NDER = 128


class _Op:
    __slots__ = ("eng", "fn", "reads", "writes", "dma", "deps", "signal", "tok", "idx", "epoch")


class Prog:
    def __init__(self, nc):
        self.nc = nc
        self.ops = []
        self.last_w = {}
        self.readers = {}
        self.dma_cnt = {}
        self.out_dmas = []
        self.epoch = 0

    def add(self, eng, fn, reads=(), writes=(), dma=None, final=False):
        op = _Op()
        op.eng, op.fn, op.reads, op.writes, op.dma = eng, fn, tuple(reads), tuple(writes), dma
        op.signal, op.tok = False, None
        op.epoch = self.epoch
        op.idx = len(self.ops)
        deps = {}
        for r in op.reads:
            w = self.last_w.get(r)
            if w is not None:
                deps[w.idx] = w
        for k in op.writes:
            w = self.last_w.get(k)
            if w is not None:
                deps[w.idx] = w
            for rd in self.readers.get(k, {}).values():
                deps[rd.idx] = rd
        rk = ("dma", op.idx) if dma is not None else eng
        for r in op.reads:
            self.readers.setdefault(r, {})[rk] = op
        for k in op.writes:
            self.readers[k] = {}
            self.last_w[k] = op
        latest = {}
        for i, dop in deps.items():
            if dop.dma is None:
                if dop.eng not in latest or latest[dop.eng].idx < i:
                    latest[dop.eng] = dop
        op.deps = tuple(sorted([dop for dop in deps.values() if dop.dma is not None] + list(latest.values()), key=lambda o: o.idx))
        if dma is not None:
            n = self.dma_cnt.get(dma, 0) + 1
            self.dma_cnt[dma] = n
            op.tok = (("dma", dma), 16 * n)
            if final:
                self.out_dmas.append(op)
        self.ops.append(op)
        return op

    def emit(self):
        nc = self.nc
        for op in self.ops:
            for d in op.deps:
                if d.dma is None:
                    if d.eng == "pe" and op.eng == "pe" and op.dma is None:
                        continue
                    d.signal = True
        cnt = {}
        for op in self.ops:
            if op.dma is None and op.signal:
                k = ("eng", op.eng, op.epoch)
                cnt[k] = cnt.get(k, 0) + 1
                op.tok = (k, cnt[k])
        with contextlib.ExitStack() as st:
            sems = {}
            for k in cnt:
                sems[k] = st.enter_context(nc.semaphore("s_%s_%d" % (k[1], k[2])))
            for i, k in enumerate(self.dma_cnt):
                sems[("dma", k)] = st.enter_context(nc.semaphore("d_%d" % i))
            block = st.enter_context(nc.Block())
            by_eng = {e: [o for o in self.ops if o.eng == e] for e in ENGS}
            out_dmas = self.out_dmas

            def run(eng_name, engine):
                seen = {}
                for op in by_eng[eng_name]:
                    for d in op.deps:
                        if d.tok is None:
                            continue
                        if d.dma is None and d.eng == "pe" and eng_name == "pe" and op.dma is None:
                            continue
                        k, v = d.tok
                        if seen.get(k, 0) >= v:
                            continue
                        seen[k] = v
                        engine.wait_ge(sems[k], v)
                    ins = op.fn(engine)
                    if op.dma is not None:
                        ins.then_inc(sems[op.tok[0]], 16)
                    elif op.signal:
                        ins.then_inc(sems[op.tok[0]], 1)
                if eng_name == "sp":
                    for op in out_dmas:
                        k, v = op.tok
                        if seen.get(k, 0) >= v:
                            continue
                        seen[k] = v
                        engine.wait_ge(sems[k], v)

            @block.tensor
            def _(e):
                run("pe", e)

            @block.scalar
            def _(e):
                run("act", e)

            @block.vector
            def _(e):
                run("dve", e)

            @block.gpsimd
            def _(e):
                run("pool", e)

            @block.sync
            def _(e):
                run("sp", e)


def build(nl, dbg=99, ntiles=NT):
    nc = bass.Bass("TRN2", target_bir_lowering=False)

    def dram(name, shape, kind):
        return nc.dram_tensor(name, shape, F32, kind=kind).ap()

    xT = dram("xT", [D, S], "ExternalInput")
    w_in = dram("w_in", [nl, D, INW], "ExternalInput")
    w_out = dram("w_out", [nl, D, D], "ExternalInput")
    w_up = dram("w_up", [nl, D, 2 * DFF], "ExternalInput")
    w_down = dram("w_down", [nl, DFF, D], "ExternalInput")
    bd = dram("bd", [nl, 128, 2048], "ExternalInput")
    vecs = dram("vecs", [nl, 128, NVL], "ExternalInput")
    dmat = dram("dmat", [128, 1152], "ExternalInput")
    y = dram("y", [D, S], "ExternalOutput")
    xs = [dram("xs0", [D, S], "Internal"), dram("xs1", [D, S], "Internal")] if nl > 1 else []

    with contextlib.ExitStack() as st:
        def sb(name, shape, dt):
            return st.enter_context(nc.sbuf_tensor(name, shape, dt))

        KT = sb("KT", [128, 8, S], BF16)
        VC = sb("VC", [128, 16, 1024], BF16)
        SL = [sb("SL0", [128, 8192], BF16), sb("SL1", [128, 8192], BF16)]
        XB = sb("XB", [128, 16, TT], BF16)
        R = sb("R", [128, 16, TT], F32)
        CC = sb("CC", [128, 16, TT], BF16)
        QH = sb("QH", [128, 8, TT], BF16)
        DM = sb("DM", [128, 1152], F32)
        ONES_LN = sb("ONES_LN", [128, 128], F32)
        ONES_SUB = sb("ONES_SUB", [128, 128], F32)
        ONES_B = sb("ONES_B", [128, 128], BF16)
        BD = sb("BD", [128, 2048], BF16)
        VEC = sb("VEC", [128, NVL], F32)
        DER = sb("DER", [128, NDER], F32)
        HRG = sb("HRG", [128, 8, 3], F32)
        HST = sb("HST", [128, 8], F32)
        HF = sb("HF", [128, 88, 2], F32)
        NF = 10
        F = [sb("F%d" % i, [128, 516], F32) for i in range(NF)]
        B = [sb("B%d" % i, [128, TT], BF16) for i in range(4)]
        PS = [st.enter_context(nc.psum_tensor("PS%d" % i, [128, TT], F32)) for i in range(8)]

        P = Prog(nc)
        A = P.add
        SL16 = [s[:].rearrange("p (k n) -> p k n", k=16) for s in SL]
        SL4 = [s[:].rearrange("p (k n) -> p k n", k=4) for s in SL]
        slab_n = [0]

        def fk(i):
            return ("F", i)

        def vcol(c, n=1):
            return VEC[:, c:c + n]

        def load_slab16(src):
            b = slab_n[0] % 2
            slab_n[0] += 1
            A("pool", lambda e: e.dma_start(out=SL16[b], in_=src.rearrange("(k p) n -> p k n", p=128)),
              writes=[("SL", b)], dma=("SL", b))
            return b

        def load_slab4(src):
            b = slab_n[0] % 2
            slab_n[0] += 1
            A("pool", lambda e: e.dma_start(out=SL4[b], in_=src.rearrange("(k p) n -> p k n", p=128)),
              writes=[("SL", b)], dma=("SL", b))
            return b

        A("sp", lambda e: e.dma_start(out=DM[:], in_=dmat), writes=["DM"], dma="dm")
        A("dve", lambda e: e.memset(ONES_LN[:], 1.0 / D), writes=["ONES"])
        A("dve", lambda e: e.memset(ONES_SUB[:], 1.0 / 128), writes=["ONES"])
        A("dve", lambda e: e.memset(ONES_B[:], 1.0), writes=["ONES"])
        D0 = DM[:, 0:512]
        DMK = DM[:, 512:1024]

        def cbias(h, k):
            c = 1024 + h * 16 + k
            return DM[:, c:c + 1]

        evac_rr = [0]

        def evac_copy(out_ap, ps_i, wkeys):
            evac_rr[0] ^= 1
            if evac_rr[0]:
                A("act", lambda e: e.activation(out=out_ap, in_=PS[ps_i][:], func=AF.Copy),
                  reads=[("PS", ps_i)], writes=wkeys)
            else:
                A("dve", lambda e: e.tensor_copy(out=out_ap, in_=PS[ps_i][:]),
                  reads=[("PS", ps_i)], writes=wkeys)

        def proj_fm(b, j, ps_i, rhs_of_kc, rkeys, nk=16):
            for kc in range(nk):
                A("pe", lambda e, kc=kc: e.matmul(PS[ps_i][:], lhsT=SL16[b][:, kc, j * 128:(j + 1) * 128],
                                                  rhs=rhs_of_kc(kc), start=(kc == 0), stop=(kc == nk - 1)),
                  reads=[("SL", b)] + rkeys, writes=[("PS", ps_i)])

        xb_keys = [("XB", m) for m in range(16)]
        cc_keys = [("CC", m) for m in range(16)]

        def layer_norm(gcol, bcol, write_xb):
            pm, pv = 6, 7
            for m in range(16):
                A("pe", lambda e, m=m: e.matmul(PS[pm][:], lhsT=ONES_LN[:], rhs=R[:, m, :], start=(m == 0), stop=(m == 15)),
                  reads=["ONES", ("R", m)], writes=[("PS", pm)])
            for m in range(16):
                A("dve", lambda e, m=m: e.tensor_tensor(out=R[:, m, :], in0=R[:, m, :], in1=PS[pm][:], op=ALU.subtract),
                  reads=[("R", m), ("PS", pm)], writes=[("R", m)])
                sq = m % 2
                A("act", lambda e, m=m, sq=sq: e.activation(out=F[sq][:, 0:TT], in_=R[:, m, :], func=AF.Square),
                  reads=[("R", m)], writes=[fk(sq)])
                A("pe", lambda e, m=m, sq=sq: e.matmul(PS[pv][:], lhsT=ONES_LN[:], rhs=F[sq][:, 0:TT], start=(m == 0), stop=(m == 15)),
                  reads=["ONES", fk(sq)], writes=[("PS", pv)])
            A("act", lambda e: e.activation(out=F[2][:, 0:TT], in_=PS[pv][:], func=AF.Sqrt, bias=vcol(V_LNEPS)),
              reads=[("PS", pv), "vecs"], writes=[fk(2)])
            A("dve", lambda e: e.reciprocal(out=F[3][:, 0:TT], in_=F[2][:, 0:TT]), reads=[fk(2)], writes=[fk(3)])
            for m in range(16):
                A("dve", lambda e, m=m: e.tensor_tensor(out=R[:, m, :], in0=R[:, m, :], in1=F[3][:, 0:TT], op=ALU.mult),
                  reads=[("R", m), fk(3)], writes=[("R", m)])
                if write_xb:
                    A("act", lambda e, m=m: e.activation(out=XB[:, m, :], in_=R[:, m, :], func=AF.Identity,
                                                         scale=vcol(gcol + m), bias=vcol(bcol + m)),
                      reads=[("R", m), "vecs"], writes=[("XB", m)])
                A("act", lambda e, m=m: e.activation(out=R[:, m, :], in_=R[:, m, :], func=AF.Identity,
                                                     scale=vcol(gcol + m), bias=vcol(bcol + m)),
                  reads=[("R", m), "vecs"], writes=[("R", m)])

        for l in range(nl):
            src = xT if l == 0 else xs[(l - 1) % 2]
            srcn = "xT" if l == 0 else "xs%d" % ((l - 1) % 2)
            dst = y if l == nl - 1 else xs[l % 2]
            dstn = "y" if l == nl - 1 else "xs%d" % (l % 2)
            A("sp", lambda e, l=l: e.dma_start(out=VEC[:], in_=vecs[l]), writes=["vecs"], dma="vecs")
            A("pool", lambda e, l=l: e.dma_start(out=BD[:], in_=bd[l]), writes=["BD"], dma="BD")
            A("dve", lambda e: e.memset(HRG[:], 0.0), writes=["HRG"])
            A("dve", lambda e: e.memset(HST[:], 0.0), writes=["HST"])
            A("dve", lambda e: e.memset(HF[:], 0.0), writes=["HF"])
            A("dve", lambda e: e.tensor_scalar(out=DER[:, DR_HBA:DR_HBA + 16], in0=VEC[:, V_BA:V_BA + 16], scalar1=0.5,
                                               scalar2=None, op0=ALU.mult), reads=["vecs"], writes=["DER"])
            A("act", lambda e: e.activation(out=DER[:, DR_T0:DR_T0 + 8], in_=VEC[:, V_LAM:V_LAM + 8], func=AF.Exp, scale=-1.0),
              reads=["vecs", "DER"], writes=["DER"])
            A("act", lambda e: e.activation(out=DER[:, DR_T0:DR_T0 + 8], in_=DER[:, DR_T0:DR_T0 + 8], func=AF.Ln, bias=vcol(V_ONE)),
              reads=["vecs", "DER"], writes=["DER"])
            A("dve", lambda e: e.tensor_scalar(out=DER[:, DR_CLH:DR_CLH + 8], in0=DER[:, DR_T0:DR_T0 + 8], scalar1=-4.0,
                                               scalar2=None, op0=ALU.mult), reads=["DER"], writes=["DER"])
            A("dve", lambda e: e.tensor_scalar(out=DER[:, DR_CL:DR_CL + 8], in0=DER[:, DR_T0:DR_T0 + 8], scalar1=-8.0,
                                               scalar2=None, op0=ALU.mult), reads=["DER"], writes=["DER"])
            for i in range(2):
                A("dve", lambda e, i=i: e.tensor_tensor(out=DER[:, DR_P:DR_P + 64], in0=VEC[:, V_LAMV + 128 * i:V_LAMV + 128 * i + 64],
                                                        in1=VEC[:, V_LAMV + 128 * i + 64:V_LAMV + 128 * i + 128], op=ALU.mult),
                  reads=["vecs", "DER"], writes=["DER"])
                A("dve", lambda e, i=i: e.tensor_reduce(out=DER[:, DR_S + i:DR_S + i + 1], in_=DER[:, DR_P:DR_P + 64], axis=AX.X, op=ALU.add),
                  reads=["DER"], writes=["DER"])
            A("act", lambda e: e.activation(out=DER[:, DR_S:DR_S + 2], in_=DER[:, DR_S:DR_S + 2], func=AF.Exp),
              reads=["DER"], writes=["DER"])
            A("dve", lambda e: e.scalar_tensor_tensor(out=DER[:, DR_NEGLAM:DR_NEGLAM + 1], in0=DER[:, DR_S + 1:DR_S + 2],
                                                      scalar=vcol(V_NLI), in1=DER[:, DR_S:DR_S + 1], op0=ALU.add, op1=ALU.subtract),
              reads=["DER", "vecs"], writes=["DER"])

            for tt in range(ntiles):
                t0 = tt * TT
                P.epoch += 1
                xsrc = src[:, t0:t0 + TT].rearrange("(m p) n -> p m n", p=128)
                A("sp", lambda e, xsrc=xsrc: e.dma_start(out=R[:], in_=xsrc), reads=[("X", srcn, tt)],
                  writes=[("R", m) for m in range(16)], dma="xld")
                A("pool", lambda e, xsrc=xsrc: e.dma_start(out=XB[:], in_=xsrc), reads=[("X", srcn, tt)],
                  writes=xb_keys, dma="xbld")
                xrhs = lambda kc: XB[:, kc, :]

                for sg in range(2 if dbg >= 1 else 0):
                    b = load_slab16(w_in[l][:, 1024 + 512 * sg:1024 + 512 * (sg + 1)])
                    for j in range(4):
                        c = 4 * sg + j
                        proj_fm(b, j, c, xrhs, xb_keys)
                        A("act", lambda e, c=c: e.activation(out=QH[:, c, :], in_=PS[c][:], func=AF.Gelu_apprx_tanh),
                          reads=[("PS", c)], writes=[("QH", c)])
                for sx in range(2 if dbg >= 2 else 0):
                    b = load_slab16(w_in[l][:, 512 * sx:512 * (sx + 1)])
                    for j in range(4):
                        proj_fm(b, j, j, xrhs, xb_keys)
                    for j in range(4):
                        c = 4 * sx + j
                        xh = c % 2
                        ub = c % 2
                        pa, px = 4 + 2 * (c % 2), 5 + 2 * (c % 2)
                        A("act", lambda e, j=j, xh=xh: e.activation(out=F[xh][:, 3:515], in_=PS[j][:], func=AF.Copy),
                          reads=[("PS", j)], writes=[fk(xh)])
                        A("dve", lambda e, c=c, xh=xh: e.tensor_copy(out=F[xh][:, 0:3], in_=HRG[:, c, :]),
                          reads=["HRG"], writes=[fk(xh)])
                        A("act", lambda e, c=c, xh=xh: e.activation(out=F[2][:, 0:TT], in_=F[xh][:, 3:515], func=AF.Identity,
                                                                    scale=vcol(V_CW + 3 * 8 + c), bias=vcol(V_CB + c)),
                          reads=[fk(xh), "vecs"], writes=[fk(2)])
                        for k in range(3):
                            A("dve", lambda e, c=c, xh=xh, k=k: e.scalar_tensor_tensor(
                                out=F[2][:, 0:TT], in0=F[xh][:, k:k + TT], scalar=vcol(V_CW + k * 8 + c), in1=F[2][:, 0:TT],
                                op0=ALU.mult, op1=ALU.add), reads=[fk(xh), fk(2), "vecs"], writes=[fk(2)])
                        A("dve", lambda e, c=c, xh=xh: e.tensor_copy(out=HRG[:, c, :], in_=F[xh][:, 512:515]),
                          reads=[fk(xh)], writes=["HRG"])
                        A("act", lambda e, ub=ub: e.activation(out=B[ub][:], in_=F[2][:, 0:TT], func=AF.Copy),
                          reads=[fk(2)], writes=[("B", ub)])
                        A("pe", lambda e, c=c, ub=ub, pa=pa: e.matmul(PS[pa][:], lhsT=BD[:, c * 128:(c + 1) * 128], rhs=B[ub][:],
                                                                     start=True, stop=True),
                          reads=["BD", ("B", ub)], writes=[("PS", pa)])
                        A("pe", lambda e, c=c, ub=ub, px=px: e.matmul(PS[px][:], lhsT=BD[:, (8 + c) * 128:(9 + c) * 128], rhs=B[ub][:],
                                                                     start=True, stop=True),
                          reads=["BD", ("B", ub)], writes=[("PS", px)])
                        A("act", lambda e, c=c, pa=pa: e.activation(out=F[3][:, 0:TT], in_=PS[pa][:], func=AF.Tanh, scale=0.5,
                                                                    bias=DER[:, DR_HBA + c:DR_HBA + c + 1]),
                          reads=[("PS", pa), "DER"], writes=[fk(3)])
                        A("act", lambda e, c=c, px=px: e.activation(out=F[4][:, 0:TT], in_=PS[px][:], func=AF.Tanh, scale=0.5,
                                                                    bias=DER[:, DR_HBX + c:DR_HBX + c + 1]),
                          reads=[("PS", px), "DER"], writes=[fk(4)])
                        A("act", lambda e, c=c: e.activation(out=F[5][:, 0:TT], in_=F[3][:, 0:TT], func=AF.Exp,
                                                             scale=DER[:, DR_CLH + c:DR_CLH + c + 1], bias=DER[:, DR_CLH + c:DR_CLH + c + 1]),
                          reads=[fk(3), "DER"], writes=[fk(5)])
                        A("act", lambda e, c=c: e.activation(out=F[6][:, 0:TT], in_=F[3][:, 0:TT], func=AF.Exp,
                                                             scale=DER[:, DR_CL + c:DR_CL + c + 1], bias=DER[:, DR_CL + c:DR_CL + c + 1]),
                          reads=[fk(3), "DER"], writes=[fk(6)])
                        A("act", lambda e: e.activation(out=F[6][:, 0:TT], in_=F[6][:, 0:TT], func=AF.Sqrt, scale=-1.0, bias=vcol(V_ONE)),
                          reads=[fk(6), "vecs"], writes=[fk(6)])
                        A("dve", lambda e: e.scalar_tensor_tensor(out=F[7][:, 0:TT], in0=F[4][:, 0:TT], scalar=1.0, in1=F[2][:, 0:TT],
                                                                  op0=ALU.add, op1=ALU.mult), reads=[fk(4), fk(2)], writes=[fk(7)])
                        A("dve", lambda e: e.scalar_tensor_tensor(out=F[7][:, 0:TT], in0=F[7][:, 0:TT], scalar=0.5, in1=F[6][:, 0:TT],
                                                                  op0=ALU.mult, op1=ALU.mult), reads=[fk(7), fk(6)], writes=[fk(7)])
                        A("dve", lambda e, c=c: e.tensor_tensor_scan(out=F[8][:, 0:TT], data0=F[5][:, 0:TT], data1=F[7][:, 0:TT],
                                                                     initial=HST[:, c:c + 1], op0=ALU.mult, op1=ALU.add),
                          reads=[fk(5), fk(7), "HST"], writes=[fk(8)])
                        A("dve", lambda e, c=c: e.tensor_copy(out=HST[:, c:c + 1], in_=F[8][:, TT - 1:TT]),
                          reads=[fk(8)], writes=["HST"])
                        A("dve", lambda e, c=c: e.tensor_tensor(out=CC[:, c, :], in0=F[8][:, 0:TT], in1=QH[:, c, :], op=ALU.mult),
                          reads=[fk(8), ("QH", c)], writes=[("CC", c)])
                for sk in range(2 if dbg >= 3 else 0):
                    b = load_slab16(w_in[l][:, 3072 + 512 * sk:3072 + 512 * (sk + 1)])
                    for j in range(4):
                        h = 4 * sk + j
                        pi = (4 * sk + j) % 8
                        proj_fm(b, j, pi, xrhs, xb_keys)
                        evac_copy(KT[:, h, t0:t0 + TT], pi, [("KT", h, tt)])
                for sv in range(2 if dbg >= 3 else 0):
                    b = load_slab16(w_in[l][:, 4096 + 512 * sv:4096 + 512 * (sv + 1)])
                    for ts in range(4):
                        pi = (4 * sv + ts) % 8
                        for kc in range(16):
                            A("pe", lambda e, kc=kc, ts=ts, pi=pi, b=b: e.matmul(
                                PS[pi][:], lhsT=XB[:, kc, ts * 128:(ts + 1) * 128], rhs=SL16[b][:, kc, :],
                                start=(kc == 0), stop=(kc == 15)),
                              reads=[("SL", b), ("XB", kc)], writes=[("PS", pi)])
                        evac_copy(VC[:, 4 * tt + ts, sv * 512:(sv + 1) * 512], pi, [("VC", 4 * tt + ts, sv)])
                for sq in range(2 if dbg >= 3 else 0):
                    b = load_slab16(w_in[l][:, 2048 + 512 * sq:2048 + 512 * (sq + 1)])
                    for j in range(4):
                        h = 4 * sq + j
                        pi = (4 * sq + j) % 8
                        proj_fm(b, j, pi, xrhs, xb_keys)
                        evac_copy(QH[:, h, :], pi, [("QH", h)])
                nkt = 4 * tt + 4
                step = 0
                for h in range(8 if dbg >= 4 else 0):
                    slope = 2.0 ** (-(h + 1))
                    for c in range(2):
                        po, pl = 2 + 2 * c, 3 + 2 * c
                        for kt in range(nkt):
                            r = kt - 4 * tt
                            n0 = 128 * r if r > 0 else 0
                            N = TT - n0
                            psi = step % 2
                            ti = step % 2
                            bi = step % 3
                            step += 1
                            A("pe", lambda e, h=h, c=c, kt=kt, n0=n0, N=N, psi=psi: e.matmul(
                                PS[psi][:, 0:N], lhsT=KT[64 * c:64 * c + 64, h, kt * 128:(kt + 1) * 128],
                                rhs=QH[64 * c:64 * c + 64, h, n0:TT], start=True, stop=True),
                              reads=[("KT", h, kt // 4), ("QH", h)], writes=[("PS", psi)])
                            dsrc = DMK if r >= 0 else D0
                            A("dve", lambda e, N=N, psi=psi, ti=ti, dsrc=dsrc, slope=slope: e.scalar_tensor_tensor(
                                out=F[ti][:, 0:N], in0=dsrc[:, 0:N], scalar=8.0 * slope, in1=PS[psi][:, 0:N],
                                op0=ALU.mult, op1=ALU.add), reads=["DM", ("PS", psi)], writes=[fk(ti)])
                            if r < 0:
                                A("act", lambda e, N=N, ti=ti, bi=bi, h=h, r=r: e.activation(
                                    out=B[bi][:, 0:N], in_=F[ti][:, 0:N], func=AF.Exp, scale=0.125, bias=cbias(h, -r)),
                                  reads=[fk(ti), "DM"], writes=[("B", bi)])
                            else:
                                A("act", lambda e, N=N, ti=ti, bi=bi: e.activation(
                                    out=B[bi][:, 0:N], in_=F[ti][:, 0:N], func=AF.Exp, scale=0.125),
                                  reads=[fk(ti)], writes=[("B", bi)])
                            A("pe", lambda e, h=h, kt=kt, n0=n0, N=N, bi=bi, po=po, nkt=nkt: e.matmul(
                                PS[po][:, n0:TT], lhsT=VC[:, kt, h * 128:(h + 1) * 128], rhs=B[bi][:, 0:N],
                                start=(kt == 0), stop=(kt == nkt - 1)),
                              reads=[("VC", kt, h // 4), ("B", bi)], writes=[("PS", po)])
                            A("pe", lambda e, n0=n0, N=N, bi=bi, pl=pl, kt=kt, nkt=nkt: e.matmul(
                                PS[pl][:, n0:TT], lhsT=ONES_B[:], rhs=B[bi][:, 0:N],
                                start=(kt == 0), stop=(kt == nkt - 1)),
                              reads=["ONES", ("B", bi)], writes=[("PS", pl)])
                        A("dve", lambda e, pl=pl: e.reciprocal(out=F[2][:, 0:TT], in_=PS[pl][:]),
                          reads=[("PS", pl)], writes=[fk(2)])
                        A("dve", lambda e, po=po, c=c: e.tensor_tensor(out=F[3 + c][:, 0:TT], in0=PS[po][:], in1=F[2][:, 0:TT], op=ALU.mult),
                          reads=[("PS", po), fk(2)], writes=[fk(3 + c)])
                    A("dve", lambda e: e.scalar_tensor_tensor(out=F[5][:, 0:TT], in0=F[4][:, 0:TT], scalar=DER[:, DR_NEGLAM:DR_NEGLAM + 1],
                                                              in1=F[3][:, 0:TT], op0=ALU.mult, op1=ALU.add),
                      reads=[fk(3), fk(4), "DER"], writes=[fk(5)])
                    A("act", lambda e: e.activation(out=F[6][:, 0:TT], in_=F[5][:, 0:TT], func=AF.Square), reads=[fk(5)], writes=[fk(6)])
                    A("pe", lambda e: e.matmul(PS[6][:], lhsT=ONES_SUB[:], rhs=F[6][:, 0:TT], start=True, stop=True),
                      reads=["ONES", fk(6)], writes=[("PS", 6)])
                    A("act", lambda e: e.activation(out=F[7][:, 0:TT], in_=PS[6][:], func=AF.Sqrt, scale=vcol(V_SUBK), bias=vcol(V_SUBE)),
                      reads=[("PS", 6), "vecs"], writes=[fk(7)])
                    A("dve", lambda e: e.reciprocal(out=F[7][:, 0:TT], in_=F[7][:, 0:TT]), reads=[fk(7)], writes=[fk(7)])
                    A("dve", lambda e, h=h: e.scalar_tensor_tensor(out=CC[:, 8 + h, :], in0=F[5][:, 0:TT], scalar=vcol(V_SUBG),
                                                                  in1=F[7][:, 0:TT], op0=ALU.mult, op1=ALU.mult),
                      reads=[fk(5), fk(7), "vecs"], writes=[("CC", 8 + h)])
                for so in range(4 if dbg >= 5 else 0):
                    b = load_slab16(w_out[l][:, 512 * so:512 * (so + 1)])
                    for j in range(4):
                        m = 4 * so + j
                        pi = m % 8
                        proj_fm(b, j, pi, lambda kc: CC[:, kc, :], cc_keys)
                        A("dve", lambda e, m=m, pi=pi: e.scalar_tensor_tensor(out=R[:, m, :], in0=R[:, m, :], scalar=ALPHA, in1=PS[pi][:],
                                                                            op0=ALU.mult, op1=ALU.add),
                          reads=[("R", m), ("PS", pi)], writes=[("R", m)])
                if dbg >= 6:
                    layer_norm(V_G1, V_B1, True)
                HB = lambda hb, j: QH[:, 4 * hb + j, :]

                def down_proj(s, bdn):
                    hb = s % 2
                    for m in range(16):
                        pi = 4 + (m % 4)
                        for kc in range(4):
                            A("pe", lambda e, m=m, kc=kc, pi=pi, hb=hb: e.matmul(
                                PS[pi][:], lhsT=SL4[bdn][:, kc, m * 128:(m + 1) * 128], rhs=HB(hb, kc),
                                start=(kc == 0), stop=(kc == 3)),
                              reads=[("SL", bdn), ("QH", 4 * hb + kc)], writes=[("PS", pi)])
                        if s == 0:
                            A("dve", lambda e, m=m, pi=pi: e.scalar_tensor_tensor(out=R[:, m, :], in0=R[:, m, :], scalar=ALPHA, in1=PS[pi][:],
                                                                                op0=ALU.mult, op1=ALU.add),
                              reads=[("R", m), ("PS", pi)], writes=[("R", m)])
                        else:
                            A("dve", lambda e, m=m, pi=pi: e.tensor_tensor(out=R[:, m, :], in0=R[:, m, :], in1=PS[pi][:], op=ALU.add),
                              reads=[("R", m), ("PS", pi)], writes=[("R", m)])

                for s in range(11 if dbg >= 7 else 0):
                    hb = s % 2
                    for half in range(2):
                        b = load_slab16(w_up[l][:, half * DFF + 512 * s:half * DFF + 512 * (s + 1)])
                        for j in range(4):
                            proj_fm(b, j, j, xrhs, xb_keys)
                        for j in range(4):
                            ch = half * 44 + 4 * s + j
                            uh = 4 + (j % 2)
                            acc = j if half == 0 else 6 + (j % 2)
                            A("act", lambda e, j=j, uh=uh: e.activation(out=F[uh][:, 2:514], in_=PS[j][:], func=AF.Copy),
                              reads=[("PS", j)], writes=[fk(uh)])
                            A("dve", lambda e, ch=ch, uh=uh: e.tensor_copy(out=F[uh][:, 0:2], in_=HF[:, ch, :]),
                              reads=["HF"], writes=[fk(uh)])
                            A("act", lambda e, ch=ch, uh=uh, acc=acc: e.activation(
                                out=F[acc][:, 0:TT], in_=F[uh][:, 2:514], func=AF.Identity,
                                scale=vcol(V_FCW + 2 * 88 + ch), bias=vcol(V_FCB + ch)),
                              reads=[fk(uh), "vecs"], writes=[fk(acc)])
                            for k in range(2):
                                A("dve", lambda e, ch=ch, uh=uh, acc=acc, k=k: e.scalar_tensor_tensor(
                                    out=F[acc][:, 0:TT], in0=F[uh][:, k:k + TT], scalar=vcol(V_FCW + k * 88 + ch), in1=F[acc][:, 0:TT],
                                    op0=ALU.mult, op1=ALU.add), reads=[fk(uh), fk(acc), "vecs"], writes=[fk(acc)])
                            A("dve", lambda e, ch=ch, uh=uh: e.tensor_copy(out=HF[:, ch, :], in_=F[uh][:, 512:514]),
                              reads=[fk(uh)], writes=["HF"])
                            if half == 0:
                                A("act", lambda e, acc=acc: e.activation(out=F[acc][:, 0:TT], in_=F[acc][:, 0:TT], func=AF.Gelu_apprx_tanh),
                                  reads=[fk(acc)], writes=[fk(acc)])
                            else:
                                A("dve", lambda e, j=j, acc=acc, hb=hb: e.tensor_tensor(out=HB(hb, j), in0=F[j][:, 0:TT], in1=F[acc][:, 0:TT], op=ALU.mult),
                                  reads=[fk(j), fk(acc)], writes=[("QH", 4 * hb + j)])
                    if s > 0:
                        bdn = load_slab4(w_down[l][512 * (s - 1):512 * s, :])
                        down_proj(s - 1, bdn)
                if dbg >= 7:
                    bdn = load_slab4(w_down[l][512 * 10:512 * 11, :])
                    down_proj(10, bdn)
                if dbg >= 8:
                    layer_norm(V_G2, V_B2, False)
                xdst = dst[:, t0:t0 + TT].rearrange("(m p) n -> p m n", p=128)
                A("sp", lambda e, xdst=xdst: e.dma_start(out=xdst, in_=R[:]), reads=[("R", m) for m in range(16)],
                  writes=[("X", dstn, tt)], dma="xst", final=(l == nl - 1))
        P.emit()
    return nc


def _pack_layer(inp, l):
    f = lambda a: np.asarray(a, np.float32)
    v = np.zeros((128, NVL), np.float32)
    pc = lambda a, n: f(a).reshape(n, 128).T
    cw = f(inp["rg_conv_w"][l])
    for k in range(4):
        v[:, V_CW + 8 * k:V_CW + 8 * k + 8] = pc(cw[k], 8)
    v[:, V_CB:V_CB + 8] = pc(inp["rg_conv_b"][l], 8)
    v[:, V_BA:V_BA + 8] = pc(inp["rg_gate_a_b"][l], 8)
    v[:, V_BX:V_BX + 8] = pc(inp["rg_gate_x_b"][l], 8)
    v[:, V_LAM:V_LAM + 8] = pc(inp["rg_lambda"][l], 8)
    v[:, V_G1:V_G1 + 16] = pc(inp["ln_mix_g"][l], 16)
    v[:, V_B1:V_B1 + 16] = pc(inp["ln_mix_b"][l], 16)
    v[:, V_G2:V_G2 + 16] = pc(inp["ln_ffn_g"][l], 16)
    v[:, V_B2:V_B2 + 16] = pc(inp["ln_ffn_b"][l], 16)
    fw = f(inp["ffn_conv_w"][l])
    for k in range(3):
        v[:, V_FCW + 88 * k:V_FCW + 88 * (k + 1)] = pc(fw[k], 88)
    v[:, V_FCB:V_FCB + 88] = pc(inp["ffn_conv_b"][l], 88)
    v[:, V_SUBG] = f(inp["subln_g"][l])
    lamv = np.concatenate([f(inp["lam_q1"][l]), f(inp["lam_k1"][l]), f(inp["lam_q2"][l]), f(inp["lam_k2"][l])])
    v[:, V_LAMV:V_LAMV + 256] = lamv[None, :]
    lam_init = 0.8 - 0.6 * math.exp(-0.3 * l)
    k2 = 1.0 / (1.0 - lam_init) ** 2
    v[:, V_NLI] = -lam_init
    v[:, V_SUBK] = k2
    v[:, V_SUBE] = 1e-5 * k2
    v[:, V_LNEPS] = 1e-5
    v[:, V_ONE] = 1.0
    bdl = np.zeros((128, 2, 8, 128), np.float32)
    for g, name in enumerate(("rg_gate_a_w", "rg_gate_x_w")):
        w = f(inp[name][l])
        for c in range(8):
            bdl[0:64, g, c, 0:64] = w[2 * c]
            bdl[64:128, g, c, 64:128] = w[2 * c + 1]
    return v, bdl.reshape(128, 2048)


def _consts():
    dm = np.zeros((128, 1152), np.float32)
    p = np.arange(128, dtype=np.float32)[:, None]
    j = np.arange(512, dtype=np.float32)[None, :]
    dm[:, 0:512] = p - j
    dm[:, 512:1024] = np.where(p - j <= 0, p - j, NEG)
    for h in range(8):
        for k in range(16):
            dm[:, 1024 + 16 * h + k] = -(2.0 ** (-(h + 1))) * 128.0 * k
    return dm


FUSED = True
_NC_CACHE = {}


def _get_nc(nl):
    if nl not in _NC_CACHE:
        _NC_CACHE[nl] = build(nl)
    return _NC_CACHE[nl]


def kernel(**inputs):
    inp = {k: np.asarray(v) for k, v in inputs.items()}
    x = np.asarray(inp["x"], np.float32)
    nb = x.shape[0]
    dm = _consts()
    packs = [_pack_layer(inp, l) for l in range(DEPTH)]
    xTs = [np.ascontiguousarray(x[b].T) for b in range(nb)]
    f = lambda a: np.ascontiguousarray(np.asarray(a, np.float32))
    if FUSED:
        nc = _get_nc(DEPTH)
        vec_all = np.stack([p[0] for p in packs])
        bd_all = np.stack([p[1] for p in packs])
        shared = {"w_in": f(inp["w_in"]), "w_out": f(inp["w_out"]), "w_up": f(inp["w_up"]), "w_down": f(inp["w_down"]),
                  "bd": bd_all, "vecs": vec_all, "dmat": dm}
        in_maps = [dict(shared, xT=xTs[b]) for b in range(nb)]
        res = run_bass_kernel_spmd(nc, in_maps, core_ids=list(range(nb)))
        outs = [res.results[b]["y"] for b in range(nb)]
    else:
        nc = _get_nc(1)
        cur = xTs
        for l in range(DEPTH):
            shared = {"w_in": f(inp["w_in"][l:l + 1]), "w_out": f(inp["w_out"][l:l + 1]), "w_up": f(inp["w_up"][l:l + 1]),
                      "w_down": f(inp["w_down"][l:l + 1]), "bd": packs[l][1][None], "vecs": packs[l][0][None], "dmat": dm}
            in_maps = [dict(shared, xT=cur[b]) for b in range(nb)]
            res = run_bass_kernel_spmd(nc, in_maps, core_ids=list(range(nb)))
            cur = [np.ascontiguousarray(res.results[b]["y"]) for b in range(nb)]
        outs = cur
    return np.stack([np.ascontiguousarray(o.T) for o in outs]).astype(np.float32)
```

```python
import contextlib
import math
import numpy as np
import concourse.bass as bass
import concourse.mybir as mybir
from concourse.bass_utils import run_bass_kernel_spmd

F32 = mybir.dt.float32
BF16 = mybir.dt.bfloat16
AF = mybir.ActivationFunctionType
ALU = mybir.AluOpType
AX = mybir.AxisListType

D = 2048
S = 2048
DEPTH = 4
DFF = 5632
INW = 5120
TT = 512
NT = S // TT
ALPHA = (2.0 * DEPTH) ** 0.25
NEG = -1.0e6
ENGS = ("pe", "act", "dve", "pool", "sp")

V_CW = 0
V_CB = 32
V_BA = 40
V_BX = 48
V_LAM = 56
V_G1 = 64
V_B1 = 80
V_G2 = 96
V_B2 = 112
V_FCW = 128
V_FCB = 392
V_SUBG = 480
V_LAMV = 481
V_NLI = 737
V_SUBK = 738
V_SUBE = 739
V_LNEPS = 740
V_ONE = 741
NVL = 742
DR_HBA = 0
DR_HBX = 8
DR_CLH = 16
DR_CL = 24
DR_T0 = 32
DR_NEGLAM = 48
DR_S = 50
DR_P = 64
NDER = 128


class _Op:
    __slots__ = ("eng", "fn", "reads", "writes", "dma", "deps", "signal", "tok", "idx", "epoch")


class Prog:
    def __init__(self, nc):
        self.nc = nc
        self.ops = []
        self.last_w = {}
        self.readers = {}
        self.dma_cnt = {}
        self.out_dmas = []
        self.epoch = 0

    def add(self, eng, fn, reads=(), writes=(), dma=None, final=False):
        op = _Op()
        op.eng, op.fn, op.reads, op.writes, op.dma = eng, fn, tuple(reads), tuple(writes), dma
        op.signal, op.tok = False, None
        op.epoch = self.epoch
        op.idx = len(self.ops)
        deps = {}
        for r in op.reads:
            w = self.last_w.get(r)
            if w is not None:
                deps[w.idx] = w
        for k in op.writes:
            w = self.last_w.get(k)
            if w is not None:
                deps[w.idx] = w
            for rd in self.readers.get(k, {}).values():
                deps[rd.idx] = rd
        rk = ("dma", op.idx) if dma is not None else eng
        for r in op.reads:
            self.readers.setdefault(r, {})[rk] = op
        for k in op.writes:
            self.readers[k] = {}
            self.last_w[k] = op
        latest = {}
        for i, dop in deps.items():
            if dop.dma is None:
                if dop.eng not in latest or latest[dop.eng].idx < i:
                    latest[dop.eng] = dop
        op.deps = tuple(sorted([dop for dop in deps.values() if dop.dma is not None] + list(latest.values()), key=lambda o: o.idx))
        if dma is not None:
            n = self.dma_cnt.get(dma, 0) + 1
            self.dma_cnt[dma] = n
            op.tok = (("dma", dma), 16 * n)
            if final:
                self.out_dmas.append(op)
        self.ops.append(op)
        return op

    def emit(self):
        nc = self.nc
        for op in self.ops:
            for d in op.deps:
                if d.dma is None:
                    if d.eng == "pe" and op.eng == "pe" and op.dma is None:
                        continue
                    d.signal = True
        cnt = {}
        for op in self.ops:
            if op.dma is None and op.signal:
                k = ("eng", op.eng, op.epoch)
                cnt[k] = cnt.get(k, 0) + 1
                op.tok = (k, cnt[k])
        with contextlib.ExitStack() as st:
            sems = {}
            for k in cnt:
                sems[k] = st.enter_context(nc.semaphore("s_%s_%d" % (k[1], k[2])))
            for i, k in enumerate(self.dma_cnt):
                sems[("dma", k)] = st.enter_context(nc.semaphore("d_%d" % i))
            block = st.enter_context(nc.Block())
            by_eng = {e: [o for o in self.ops if o.eng == e] for e in ENGS}
            out_dmas = self.out_dmas

            def run(eng_name, engine):
                seen = {}
                for op in by_eng[eng_name]:
                    for d in op.deps:
                        if d.tok is None:
                            continue
                        if d.dma is None and d.eng == "pe" and eng_name == "pe" and op.dma is None:
                            continue
                        k, v = d.tok
                        if seen.get(k, 0) >= v:
                            continue
                        seen[k] = v
                        engine.wait_ge(sems[k], v)
                    ins = op.fn(engine)
                    if op.dma is not None:
                        ins.then_inc(sems[op.tok[0]], 16)
                    elif op.signal:
                        ins.then_inc(sems[op.tok[0]], 1)
                if eng_name == "sp":
                    for op in out_dmas:
                        k, v = op.tok
                        if seen.get(k, 0) >= v:
                            continue
                        seen[k] = v
                        engine.wait_ge(sems[k], v)

            @block.tensor
            def _(e):
                run("pe", e)

            @block.scalar
            def _(e):
                run("act", e)

            @block.vector
            def _(e):
                run("dve", e)

            @block.gpsimd
            def _(e):
                run("pool", e)

            @block.sync
            def _(e):
                run("sp", e)


def build(nl, dbg=99, ntiles=NT):
    nc = bass.Bass("TRN2", target_bir_lowering=False)

    def dram(name, shape, kind):
        return nc.dram_tensor(name, shape, F32, kind=kind).ap()

    xT = dram("xT", [D, S], "ExternalInput")
    w_in = dram("w_in", [nl, D, INW], "ExternalInput")
    w_out = dram("w_out", [nl, D, D], "ExternalInput")
    w_up = dram("w_up", [nl, D, 2 * DFF], "ExternalInput")
    w_down = dram("w_down", [nl, DFF, D], "ExternalInput")
    bd = dram("bd", [nl, 128, 2048], "ExternalInput")
    vecs = dram("vecs", [nl, 128, NVL], "ExternalInput")
    dmat = dram("dmat", [128, 1152], "ExternalInput")
    y = dram("y", [D, S], "ExternalOutput")
    xs = [dram("xs0", [D, S], "Internal"), dram("xs1", [D, S], "Internal")] if nl > 1 else []

    with contextlib.ExitStack() as st:
        def sb(name, shape, dt):
            return st.enter_context(nc.sbuf_tensor(name, shape, dt))

        KT = sb("KT", [128, 8, S], BF16)
        VC = sb("VC", [128, 16, 1024], BF16)
        SL = [sb("SL0", [128, 8192], BF16), sb("SL1", [128, 8192], BF16)]
        XB = sb("XB", [128, 16, TT], BF16)
        R = sb("R", [128, 16, TT], F32)
        CC = sb("CC", [128, 16, TT], BF16)
        QH = sb("QH", [128, 8, TT], BF16)
        DM = sb("DM", [128, 1152], F32)
        ONES_LN = sb("ONES_LN", [128, 128], F32)
        ONES_SUB = sb("ONES_SUB", [128, 128], F32)
        ONES_B = sb("ONES_B", [128, 128], BF16)
        BD = sb("BD", [128, 2048], BF16)
        VEC = sb("VEC", [128, NVL], F32)
        DER = sb("DER", [128, NDER], F32)
        HRG = sb("HRG", [128, 8, 3], F32)
        HST = sb("HST", [128, 8], F32)
        HF = sb("HF", [128, 88, 2], F32)
        NF = 9
        F = [sb("F%d" % i, [128, 516], F32) for i in range(NF)]
        B = [sb("B%d" % i, [128, TT], BF16) for i in range(4)]
        QZ = [sb("QZ%d" % i, [128, TT], BF16) for i in range(2)]
        PS = [st.enter_context(nc.psum_tensor("PS%d" % i, [128, TT], F32)) for i in range(8)]

        P = Prog(nc)
        A = P.add
        SL16 = [s[:].rearrange("p (k n) -> p k n", k=16) for s in SL]
        SL4 = [s[:].rearrange("p (k n) -> p k n", k=4) for s in SL]
        slab_n = [0]

        def fk(i):
            return ("F", i)

        def vcol(c, n=1):
            return VEC[:, c:c + n]

        def load_slab16(src):
            b = slab_n[0] % 2
            slab_n[0] += 1
            A("pool", lambda e: e.dma_start(out=SL16[b], in_=src.rearrange("(k p) n -> p k n", p=128)),
              writes=[("SL", b)], dma=("SL", b))
            return b

        def load_slab4(src):
            b = slab_n[0] % 2
            slab_n[0] += 1
            A("pool", lambda e: e.dma_start(out=SL4[b], in_=src.rearrange("(k p) n -> p k n", p=128)),
              writes=[("SL", b)], dma=("SL", b))
            return b

        A("sp", lambda e: e.dma_start(out=DM[:], in_=dmat), writes=["DM"], dma="dm")
        A("dve", lambda e: e.memset(ONES_LN[:], 1.0 / D), writes=["ONES"])
        A("dve", lambda e: e.memset(ONES_SUB[:], 1.0 / 128), writes=["ONES"])
        A("dve", lambda e: e.memset(ONES_B[:], 1.0), writes=["ONES"])
        for qi in range(2):
            A("dve", lambda e, qi=qi: e.memset(QZ[qi][:], 0.0), writes=[("QZ", qi)])
        D0 = DM[:, 0:512]
        DMK = DM[:, 512:1024]

        def cbias(h, k):
            c = 1024 + h * 16 + k
            return DM[:, c:c + 1]

        evac_rr = [0]

        def evac_copy(out_ap, ps_i, wkeys):
            evac_rr[0] ^= 1
            if evac_rr[0]:
                A("act", lambda e: e.activation(out=out_ap, in_=PS[ps_i][:], func=AF.Copy),
                  reads=[("PS", ps_i)], writes=wkeys)
            else:
                A("dve", lambda e: e.tensor_copy(out=out_ap, in_=PS[ps_i][:]),
                  reads=[("PS", ps_i)], writes=wkeys)

        def proj_fm(b, j, ps_i, rhs_of_kc, rkeys, nk=16):
            for kc in range(nk):
                A("pe", lambda e, kc=kc: e.matmul(PS[ps_i][:], lhsT=SL16[b][:, kc, j * 128:(j + 1) * 128],
                                                  rhs=rhs_of_kc(kc), start=(kc == 0), stop=(kc == nk - 1)),
                  reads=[("SL", b)] + rkeys, writes=[("PS", ps_i)])

        xb_keys = [("XB", m) for m in range(16)]
        cc_keys = [("CC", m) for m in range(16)]

        def layer_norm(gcol, bcol, write_xb):
            pm, pv = 6, 7
            for m in range(16):
                A("pe", lambda e, m=m: e.matmul(PS[pm][:], lhsT=ONES_LN[:], rhs=R[:, m, :], start=(m == 0), stop=(m == 15)),
                  reads=["ONES", ("R", m)], writes=[("PS", pm)])
            for m in range(16):
                A("dve", lambda e, m=m: e.tensor_tensor(out=R[:, m, :], in0=R[:, m, :], in1=PS[pm][:], op=ALU.subtract),
                  reads=[("R", m), ("PS", pm)], writes=[("R", m)])
                sq = m % 2
                A("act", lambda e, m=m, sq=sq: e.activation(out=F[sq][:, 0:TT], in_=R[:, m, :], func=AF.Square),
                  reads=[("R", m)], writes=[fk(sq)])
                A("pe", lambda e, m=m, sq=sq: e.matmul(PS[pv][:], lhsT=ONES_LN[:], rhs=F[sq][:, 0:TT], start=(m == 0), stop=(m == 15)),
                  reads=["ONES", fk(sq)], writes=[("PS", pv)])
            A("act", lambda e: e.activation(out=F[2][:, 0:TT], in_=PS[pv][:], func=AF.Sqrt, bias=vcol(V_LNEPS)),
              reads=[("PS", pv), "vecs"], writes=[fk(2)])
            A("dve", lambda e: e.reciprocal(out=F[3][:, 0:TT], in_=F[2][:, 0:TT]), reads=[fk(2)], writes=[fk(3)])
            for m in range(16):
                A("dve", lambda e, m=m: e.tensor_tensor(out=R[:, m, :], in0=R[:, m, :], in1=F[3][:, 0:TT], op=ALU.mult),
                  reads=[("R", m), fk(3)], writes=[("R", m)])
                if write_xb:
                    A("act", lambda e, m=m: e.activation(out=XB[:, m, :], in_=R[:, m, :], func=AF.Identity,
                                                         scale=vcol(gcol + m), bias=vcol(bcol + m)),
                      reads=[("R", m), "vecs"], writes=[("XB", m)])
                A("act", lambda e, m=m: e.activation(out=R[:, m, :], in_=R[:, m, :], func=AF.Identity,
                                                     scale=vcol(gcol + m), bias=vcol(bcol + m)),
                  reads=[("R", m), "vecs"], writes=[("R", m)])

        for l in range(nl):
            src = xT if l == 0 else xs[(l - 1) % 2]
            srcn = "xT" if l == 0 else "xs%d" % ((l - 1) % 2)
            dst = y if l == nl - 1 else xs[l % 2]
            dstn = "y" if l == nl - 1 else "xs%d" % (l % 2)
            A("sp", lambda e, l=l: e.dma_start(out=VEC[:], in_=vecs[l]), writes=["vecs"], dma="vecs")
            A("pool", lambda e, l=l: e.dma_start(out=BD[:], in_=bd[l]), writes=["BD"], dma="BD")
            A("dve", lambda e: e.memset(HRG[:], 0.0), writes=["HRG"])
            A("dve", lambda e: e.memset(HST[:], 0.0), writes=["HST"])
            A("dve", lambda e: e.memset(HF[:], 0.0), writes=["HF"])
            A("dve", lambda e: e.tensor_scalar(out=DER[:, DR_HBA:DR_HBA + 16], in0=VEC[:, V_BA:V_BA + 16], scalar1=0.5,
                                               scalar2=None, op0=ALU.mult), reads=["vecs"], writes=["DER"])
            A("act", lambda e: e.activation(out=DER[:, DR_T0:DR_T0 + 8], in_=VEC[:, V_LAM:V_LAM + 8], func=AF.Exp, scale=-1.0),
              reads=["vecs", "DER"], writes=["DER"])
            A("act", lambda e: e.activation(out=DER[:, DR_T0:DR_T0 + 8], in_=DER[:, DR_T0:DR_T0 + 8], func=AF.Ln, bias=vcol(V_ONE)),
              reads=["vecs", "DER"], writes=["DER"])
            A("dve", lambda e: e.tensor_scalar(out=DER[:, DR_CLH:DR_CLH + 8], in0=DER[:, DR_T0:DR_T0 + 8], scalar1=-4.0,
                                               scalar2=None, op0=ALU.mult), reads=["DER"], writes=["DER"])
            A("dve", lambda e: e.tensor_scalar(out=DER[:, DR_CL:DR_CL + 8], in0=DER[:, DR_T0:DR_T0 + 8], scalar1=-8.0,
                                               scalar2=None, op0=ALU.mult), reads=["DER"], writes=["DER"])
            for i in range(2):
                A("dve", lambda e, i=i: e.tensor_tensor(out=DER[:, DR_P:DR_P + 64], in0=VEC[:, V_LAMV + 128 * i:V_LAMV + 128 * i + 64],
                                                        in1=VEC[:, V_LAMV + 128 * i + 64:V_LAMV + 128 * i + 128], op=ALU.mult),
                  reads=["vecs", "DER"], writes=["DER"])
                A("dve", lambda e, i=i: e.tensor_reduce(out=DER[:, DR_S + i:DR_S + i + 1], in_=DER[:, DR_P:DR_P + 64], axis=AX.X, op=ALU.add),
                  reads=["DER"], writes=["DER"])
            A("act", lambda e: e.activation(out=DER[:, DR_S:DR_S + 2], in_=DER[:, DR_S:DR_S + 2], func=AF.Exp),
              reads=["DER"], writes=["DER"])
            A("dve", lambda e: e.scalar_tensor_tensor(out=DER[:, DR_NEGLAM:DR_NEGLAM + 1], in0=DER[:, DR_S + 1:DR_S + 2],
                                                      scalar=vcol(V_NLI), in1=DER[:, DR_S:DR_S + 1], op0=ALU.add, op1=ALU.subtract),
              reads=["DER", "vecs"], writes=["DER"])

            for tt in range(ntiles):
                t0 = tt * TT
                P.epoch += 1
                xsrc = src[:, t0:t0 + TT].rearrange("(m p) n -> p m n", p=128)
                A("sp", lambda e, xsrc=xsrc: e.dma_start(out=R[:], in_=xsrc), reads=[("X", srcn, tt)],
                  writes=[("R", m) for m in range(16)], dma="xld")
                A("pool", lambda e, xsrc=xsrc: e.dma_start(out=XB[:], in_=xsrc), reads=[("X", srcn, tt)],
                  writes=xb_keys, dma="xbld")
                xrhs = lambda kc: XB[:, kc, :]

                for sg in range(2 if dbg >= 1 else 0):
                    b = load_slab16(w_in[l][:, 1024 + 512 * sg:1024 + 512 * (sg + 1)])
                    for j in range(4):
                        c = 4 * sg + j
                        proj_fm(b, j, c, xrhs, xb_keys)
                        A("act", lambda e, c=c: e.activation(out=QH[:, c, :], in_=PS[c][:], func=AF.Gelu_apprx_tanh),
                          reads=[("PS", c)], writes=[("QH", c)])
                for sx in range(2 if dbg >= 2 else 0):
                    b = load_slab16(w_in[l][:, 512 * sx:512 * (sx + 1)])
                    for j in range(4):
                        proj_fm(b, j, j, xrhs, xb_keys)
                    for j in range(4):
                        c = 4 * sx + j
                        xh = c % 2
                        ub = c % 2
                        pa, px = 4 + 2 * (c % 2), 5 + 2 * (c % 2)
                        A("act", lambda e, j=j, xh=xh: e.activation(out=F[xh][:, 3:515], in_=PS[j][:], func=AF.Copy),
                          reads=[("PS", j)], writes=[fk(xh)])
                        A("dve", lambda e, c=c, xh=xh: e.tensor_copy(out=F[xh][:, 0:3], in_=HRG[:, c, :]),
                          reads=["HRG"], writes=[fk(xh)])
                        A("act", lambda e, c=c, xh=xh: e.activation(out=F[2][:, 0:TT], in_=F[xh][:, 3:515], func=AF.Identity,
                                                                    scale=vcol(V_CW + 3 * 8 + c), bias=vcol(V_CB + c)),
                          reads=[fk(xh), "vecs"], writes=[fk(2)])
                        for k in range(3):
                            A("dve", lambda e, c=c, xh=xh, k=k: e.scalar_tensor_tensor(
                                out=F[2][:, 0:TT], in0=F[xh][:, k:k + TT], scalar=vcol(V_CW + k * 8 + c), in1=F[2][:, 0:TT],
                                op0=ALU.mult, op1=ALU.add), reads=[fk(xh), fk(2), "vecs"], writes=[fk(2)])
                        A("dve", lambda e, c=c, xh=xh: e.tensor_copy(out=HRG[:, c, :], in_=F[xh][:, 512:515]),
                          reads=[fk(xh)], writes=["HRG"])
                        A("act", lambda e, ub=ub: e.activation(out=B[ub][:], in_=F[2][:, 0:TT], func=AF.Copy),
                          reads=[fk(2)], writes=[("B", ub)])
                        A("pe", lambda e, c=c, ub=ub, pa=pa: e.matmul(PS[pa][:], lhsT=BD[:, c * 128:(c + 1) * 128], rhs=B[ub][:],
                                                                     start=True, stop=True),
                          reads=["BD", ("B", ub)], writes=[("PS", pa)])
                        A("pe", lambda e, c=c, ub=ub, px=px: e.matmul(PS[px][:], lhsT=BD[:, (8 + c) * 128:(9 + c) * 128], rhs=B[ub][:],
                                                                     start=True, stop=True),
                          reads=["BD", ("B", ub)], writes=[("PS", px)])
                        A("act", lambda e, c=c, pa=pa: e.activation(out=F[3][:, 0:TT], in_=PS[pa][:], func=AF.Tanh, scale=0.5,
                                                                    bias=DER[:, DR_HBA + c:DR_HBA + c + 1]),
                          reads=[("PS", pa), "DER"], writes=[fk(3)])
                        A("act", lambda e, c=c, px=px: e.activation(out=F[4][:, 0:TT], in_=PS[px][:], func=AF.Tanh, scale=0.5,
                                                                    bias=DER[:, DR_HBX + c:DR_HBX + c + 1]),
                          reads=[("PS", px), "DER"], writes=[fk(4)])
                        A("act", lambda e, c=c: e.activation(out=F[5][:, 0:TT], in_=F[3][:, 0:TT], func=AF.Exp,
                                                             scale=DER[:, DR_CLH + c:DR_CLH + c + 1], bias=DER[:, DR_CLH + c:DR_CLH + c + 1]),
                          reads=[fk(3), "DER"], writes=[fk(5)])
                        A("act", lambda e, c=c: e.activation(out=F[6][:, 0:TT], in_=F[3][:, 0:TT], func=AF.Exp,
                                                             scale=DER[:, DR_CL + c:DR_CL + c + 1], bias=DER[:, DR_CL + c:DR_CL + c + 1]),
                          reads=[fk(3), "DER"], writes=[fk(6)])
                        A("act", lambda e: e.activation(out=F[6][:, 0:TT], in_=F[6][:, 0:TT], func=AF.Sqrt, scale=-1.0, bias=vcol(V_ONE)),
                          reads=[fk(6), "vecs"], writes=[fk(6)])
                        A("dve", lambda e: e.scalar_tensor_tensor(out=F[7][:, 0:TT], in0=F[4][:, 0:TT], scalar=1.0, in1=F[2][:, 0:TT],
                                                                  op0=ALU.add, op1=ALU.mult), reads=[fk(4), fk(2)], writes=[fk(7)])
                        A("dve", lambda e: e.scalar_tensor_tensor(out=F[7][:, 0:TT], in0=F[7][:, 0:TT], scalar=0.5, in1=F[6][:, 0:TT],
                                                                  op0=ALU.mult, op1=ALU.mult), reads=[fk(7), fk(6)], writes=[fk(7)])
                        A("dve", lambda e, c=c: e.tensor_tensor_scan(out=F[8][:, 0:TT], data0=F[5][:, 0:TT], data1=F[7][:, 0:TT],
                                                                     initial=HST[:, c:c + 1], op0=ALU.mult, op1=ALU.add),
                          reads=[fk(5), fk(7), "HST"], writes=[fk(8)])
                        A("dve", lambda e, c=c: e.tensor_copy(out=HST[:, c:c + 1], in_=F[8][:, TT - 1:TT]),
                          reads=[fk(8)], writes=["HST"])
                        A("dve", lambda e, c=c: e.tensor_tensor(out=CC[:, c, :], in0=F[8][:, 0:TT], in1=QH[:, c, :], op=ALU.mult),
                          reads=[fk(8), ("QH", c)], writes=[("CC", c)])
                for sk in range(2 if dbg >= 3 else 0):
                    b = load_slab16(w_in[l][:, 3072 + 512 * sk:3072 + 512 * (sk + 1)])
                    for j in range(4):
                        h = 4 * sk + j
                        pi = (4 * sk + j) % 8
                        proj_fm(b, j, pi, xrhs, xb_keys)
                        evac_copy(KT[:, h, t0:t0 + TT], pi, [("KT", h, tt)])
                for sv in range(2 if dbg >= 3 else 0):
                    b = load_slab16(w_in[l][:, 4096 + 512 * sv:4096 + 512 * (sv + 1)])
                    for ts in range(4):
                        pi = (4 * sv + ts) % 8
                        for kc in range(16):
                            A("pe", lambda e, kc=kc, ts=ts, pi=pi, b=b: e.matmul(
                                PS[pi][:], lhsT=XB[:, kc, ts * 128:(ts + 1) * 128], rhs=SL16[b][:, kc, :],
                                start=(kc == 0), stop=(kc == 15)),
                              reads=[("SL", b), ("XB", kc)], writes=[("PS", pi)])
                        evac_copy(VC[:, 4 * tt + ts, sv * 512:(sv + 1) * 512], pi, [("VC", 4 * tt + ts, sv)])
                for sq in range(2 if dbg >= 3 else 0):
                    b = load_slab16(w_in[l][:, 2048 + 512 * sq:2048 + 512 * (sq + 1)])
                    for j in range(4):
                        h = 4 * sq + j
                        pi = (4 * sq + j) % 8
                        proj_fm(b, j, pi, xrhs, xb_keys)
                        evac_copy(QH[:, h, :], pi, [("QH", h)])
                nkt = 4 * tt + 4
                steps = [(h, c, kt) for h in range(8 if dbg >= 4 else 0) for c in range(2) for kt in range(nkt)]
                LA = 2
                SBK = (0, 1, 7)
                deferred = []

                def emit_qk(i):
                    h, c, kt = steps[i]
                    slope = 2.0 ** (-(h + 1))
                    r = kt - 4 * tt
                    n0 = 128 * r if r > 0 else 0
                    N = TT - n0
                    psi = SBK[i % 3]
                    ti = i % 2
                    bi = i % 4
                    qi = c
                    if kt == 0:
                        if c == 0:
                            A("act", lambda e, h=h, qi=qi: e.activation(out=QZ[qi][0:64, :], in_=QH[0:64, h, :], func=AF.Copy),
                              reads=[("QH", h)], writes=[("QZ", qi)])
                        else:
                            A("dve", lambda e, h=h, qi=qi: e.tensor_copy(out=QZ[qi][64:128, :], in_=QH[64:128, h, :]),
                              reads=[("QH", h)], writes=[("QZ", qi)])
                    A("pe", lambda e, h=h, kt=kt, n0=n0, N=N, psi=psi, qi=qi: e.matmul(
                        PS[psi][:, 0:N], lhsT=KT[:, h, kt * 128:(kt + 1) * 128], rhs=QZ[qi][:, n0:TT], start=True, stop=True),
                      reads=[("KT", h, kt // 4), ("QZ", qi)], writes=[("PS", psi)])
                    dsrc = DMK if r >= 0 else D0
                    A("dve", lambda e, N=N, psi=psi, ti=ti, dsrc=dsrc, slope=slope: e.scalar_tensor_tensor(
                        out=F[ti][:, 0:N], in0=dsrc[:, 0:N], scalar=8.0 * slope, in1=PS[psi][:, 0:N],
                        op0=ALU.mult, op1=ALU.add), reads=["DM", ("PS", psi)], writes=[fk(ti)])
                    if r < 0:
                        A("act", lambda e, N=N, ti=ti, bi=bi, h=h, r=r: e.activation(
                            out=B[bi][:, 0:N], in_=F[ti][:, 0:N], func=AF.Exp, scale=0.125, bias=cbias(h, -r)),
                          reads=[fk(ti), "DM"], writes=[("B", bi)])
                    else:
                        A("act", lambda e, N=N, ti=ti, bi=bi: e.activation(
                            out=B[bi][:, 0:N], in_=F[ti][:, 0:N], func=AF.Exp, scale=0.125),
                          reads=[fk(ti)], writes=[("B", bi)])

                def head_tail(h):
                    A("pe", lambda e: e.matmul(PS[6][:], lhsT=ONES_SUB[:], rhs=F[6][:, 0:TT], start=True, stop=True),
                      reads=["ONES", fk(6)], writes=[("PS", 6)])
                    A("act", lambda e: e.activation(out=F[7][:, 0:TT], in_=PS[6][:], func=AF.Sqrt, scale=vcol(V_SUBK), bias=vcol(V_SUBE)),
                      reads=[("PS", 6), "vecs"], writes=[fk(7)])
                    A("dve", lambda e: e.reciprocal(out=F[7][:, 0:TT], in_=F[7][:, 0:TT]), reads=[fk(7)], writes=[fk(7)])
                    A("dve", lambda e, h=h: e.scalar_tensor_tensor(out=CC[:, 8 + h, :], in0=F[5][:, 0:TT], scalar=vcol(V_SUBG),
                                                                  in1=F[7][:, 0:TT], op0=ALU.mult, op1=ALU.mult),
                      reads=[fk(5), fk(7), "vecs"], writes=[("CC", 8 + h)])

                def emit_pv(i):
                    h, c, kt = steps[i]
                    r = kt - 4 * tt
                    n0 = 128 * r if r > 0 else 0
                    N = TT - n0
                    bi = i % 4
                    po, pl = 2 + 2 * c, 3 + 2 * c
                    A("pe", lambda e, h=h, kt=kt, n0=n0, N=N, bi=bi, po=po: e.matmul(
                        PS[po][:, n0:TT], lhsT=VC[:, kt, h * 128:(h + 1) * 128], rhs=B[bi][:, 0:N],
                        start=(kt == 0), stop=(kt == nkt - 1)),
                      reads=[("VC", kt, h // 4), ("B", bi)], writes=[("PS", po)])
                    A("pe", lambda e, n0=n0, N=N, bi=bi, pl=pl, kt=kt: e.matmul(
                        PS[pl][:, n0:TT], lhsT=ONES_B[:], rhs=B[bi][:, 0:N],
                        start=(kt == 0), stop=(kt == nkt - 1)),
                      reads=["ONES", ("B", bi)], writes=[("PS", pl)])
                    if kt == nkt - 1:
                        A("dve", lambda e, pl=pl: e.reciprocal(out=F[2][:, 0:TT], in_=PS[pl][:]),
                          reads=[("PS", pl)], writes=[fk(2)])
                        A("dve", lambda e, po=po, c=c: e.tensor_tensor(out=F[3 + c][:, 0:TT], in0=PS[po][:], in1=F[2][:, 0:TT], op=ALU.mult),
                          reads=[("PS", po), fk(2)], writes=[fk(3 + c)])
                        if c == 1:
                            A("dve", lambda e: e.scalar_tensor_tensor(out=F[5][:, 0:TT], in0=F[4][:, 0:TT], scalar=DER[:, DR_NEGLAM:DR_NEGLAM + 1],
                                                                      in1=F[3][:, 0:TT], op0=ALU.mult, op1=ALU.add),
                              reads=[fk(3), fk(4), "DER"], writes=[fk(5)])
                            A("act", lambda e: e.activation(out=F[6][:, 0:TT], in_=F[5][:, 0:TT], func=AF.Square), reads=[fk(5)], writes=[fk(6)])
                            deferred.append((i + 4, h))

                for i in range(len(steps) + LA):
                    if i < len(steps):
                        emit_qk(i)
                    if i >= LA:
                        emit_pv(i - LA)
                    while deferred and deferred[0][0] <= i - LA:
                        head_tail(deferred.pop(0)[1])
                while deferred:
                    head_tail(deferred.pop(0)[1])
                for so in range(4 if dbg >= 5 else 0):
                    b = load_slab16(w_out[l][:, 512 * so:512 * (so + 1)])
                    for j in range(4):
                        m = 4 * so + j
                        pi = m % 8
                        proj_fm(b, j, pi, lambda kc: CC[:, kc, :], cc_keys)
                        A("dve", lambda e, m=m, pi=pi: e.scalar_tensor_tensor(out=R[:, m, :], in0=R[:, m, :], scalar=ALPHA, in1=PS[pi][:],
                                                                            op0=ALU.mult, op1=ALU.add),
                          reads=[("R", m), ("PS", pi)], writes=[("R", m)])
                if dbg >= 6:
                    layer_norm(V_G1, V_B1, True)
                HB = lambda hb, j: QH[:, 4 * hb + j, :]

                def down_proj(s, bdn):
                    hb = s % 2
                    for m in range(16):
                        pi = 4 + (m % 4)
                        for kc in range(4):
                            A("pe", lambda e, m=m, kc=kc, pi=pi, hb=hb: e.matmul(
                                PS[pi][:], lhsT=SL4[bdn][:, kc, m * 128:(m + 1) * 128], rhs=HB(hb, kc),
                                start=(kc == 0), stop=(kc == 3)),
                              reads=[("SL", bdn), ("QH", 4 * hb + kc)], writes=[("PS", pi)])
                        if s == 0:
                            A("dve", lambda e, m=m, pi=pi: e.scalar_tensor_tensor(out=R[:, m, :], in0=R[:, m, :], scalar=ALPHA, in1=PS[pi][:],
                                                                                op0=ALU.mult, op1=ALU.add),
                              reads=[("R", m), ("PS", pi)], writes=[("R", m)])
                        else:
                            A("dve", lambda e, m=m, pi=pi: e.tensor_tensor(out=R[:, m, :], in0=R[:, m, :], in1=PS[pi][:], op=ALU.add),
                              reads=[("R", m), ("PS", pi)], writes=[("R", m)])

                for s in range(11 if dbg >= 7 else 0):
                    hb = s % 2
                    for half in range(2):
                        b = load_slab16(w_up[l][:, half * DFF + 512 * s:half * DFF + 512 * (s + 1)])
                        for j in range(4):
                            proj_fm(b, j, j, xrhs, xb_keys)
                        for j in range(4):
                            ch = half * 44 + 4 * s + j
                            uh = 4 + (j % 2)
                            acc = j if half == 0 else 6 + (j % 2)
                            A("act", lambda e, j=j, uh=uh: e.activation(out=F[uh][:, 2:514], in_=PS[j][:], func=AF.Copy),
                              reads=[("PS", j)], writes=[fk(uh)])
                            A("dve", lambda e, ch=ch, uh=uh: e.tensor_copy(out=F[uh][:, 0:2], in_=HF[:, ch, :]),
                              reads=["HF"], writes=[fk(uh)])
                            A("act", lambda e, ch=ch, uh=uh, acc=acc: e.activation(
                                out=F[acc][:, 0:TT], in_=F[uh][:, 2:514], func=AF.Identity,
                                scale=vcol(V_FCW + 2 * 88 + ch), bias=vcol(V_FCB + ch)),
                              reads=[fk(uh), "vecs"], writes=[fk(acc)])
                            for k in range(2):
                                A("dve", lambda e, ch=ch, uh=uh, acc=acc, k=k: e.scalar_tensor_tensor(
                                    out=F[acc][:, 0:TT], in0=F[uh][:, k:k + TT], scalar=vcol(V_FCW + k * 88 + ch), in1=F[acc][:, 0:TT],
                                    op0=ALU.mult, op1=ALU.add), reads=[fk(uh), fk(acc), "vecs"], writes=[fk(acc)])
                            A("dve", lambda e, ch=ch, uh=uh: e.tensor_copy(out=HF[:, ch, :], in_=F[uh][:, 512:514]),
                              reads=[fk(uh)], writes=["HF"])
                            if half == 0:
                                A("act", lambda e, acc=acc: e.activation(out=F[acc][:, 0:TT], in_=F[acc][:, 0:TT], func=AF.Gelu_apprx_tanh),
                                  reads=[fk(acc)], writes=[fk(acc)])
                            else:
                                A("dve", lambda e, j=j, acc=acc, hb=hb: e.tensor_tensor(out=HB(hb, j), in0=F[j][:, 0:TT], in1=F[acc][:, 0:TT], op=ALU.mult),
                                  reads=[fk(j), fk(acc)], writes=[("QH", 4 * hb + j)])
                    if s > 0:
                        bdn = load_slab4(w_down[l][512 * (s - 1):512 * s, :])
                        down_proj(s - 1, bdn)
                if dbg >= 7:
                    bdn = load_slab4(w_down[l][512 * 10:512 * 11, :])
                    down_proj(10, bdn)
                if dbg >= 8:
                    layer_norm(V_G2, V_B2, False)
                xdst = dst[:, t0:t0 + TT].rearrange("(m p) n -> p m n", p=128)
                A("sp", lambda e, xdst=xdst: e.dma_start(out=xdst, in_=R[:]), reads=[("R", m) for m in range(16)],
                  writes=[("X", dstn, tt)], dma="xst", final=(l == nl - 1))
        P.emit()
    return nc


def _pack_layer(inp, l):
    f = lambda a: np.asarray(a, np.float32)
    v = np.zeros((128, NVL), np.float32)
    pc = lambda a, n: f(a).reshape(n, 128).T
    cw = f(inp["rg_conv_w"][l])
    for k in range(4):
        v[:, V_CW + 8 * k:V_CW + 8 * k + 8] = pc(cw[k], 8)
    v[:, V_CB:V_CB + 8] = pc(inp["rg_conv_b"][l], 8)
    v[:, V_BA:V_BA + 8] = pc(inp["rg_gate_a_b"][l], 8)
    v[:, V_BX:V_BX + 8] = pc(inp["rg_gate_x_b"][l], 8)
    v[:, V_LAM:V_LAM + 8] = pc(inp["rg_lambda"][l], 8)
    v[:, V_G1:V_G1 + 16] = pc(inp["ln_mix_g"][l], 16)
    v[:, V_B1:V_B1 + 16] = pc(inp["ln_mix_b"][l], 16)
    v[:, V_G2:V_G2 + 16] = pc(inp["ln_ffn_g"][l], 16)
    v[:, V_B2:V_B2 + 16] = pc(inp["ln_ffn_b"][l], 16)
    fw = f(inp["ffn_conv_w"][l])
    for k in range(3):
        v[:, V_FCW + 88 * k:V_FCW + 88 * (k + 1)] = pc(fw[k], 88)
    v[:, V_FCB:V_FCB + 88] = pc(inp["ffn_conv_b"][l], 88)
    v[:, V_SUBG] = f(inp["subln_g"][l])
    lamv = np.concatenate([f(inp["lam_q1"][l]), f(inp["lam_k1"][l]), f(inp["lam_q2"][l]), f(inp["lam_k2"][l])])
    v[:, V_LAMV:V_LAMV + 256] = lamv[None, :]
    lam_init = 0.8 - 0.6 * math.exp(-0.3 * l)
    k2 = 1.0 / (1.0 - lam_init) ** 2
    v[:, V_NLI] = -lam_init
    v[:, V_SUBK] = k2
    v[:, V_SUBE] = 1e-5 * k2
    v[:, V_LNEPS] = 1e-5
    v[:, V_ONE] = 1.0
    bdl = np.zeros((128, 2, 8, 128), np.float32)
    for g, name in enumerate(("rg_gate_a_w", "rg_gate_x_w")):
        w = f(inp[name][l])
        for c in range(8):
            bdl[0:64, g, c, 0:64] = w[2 * c]
            bdl[64:128, g, c, 64:128] = w[2 * c + 1]
    return v, bdl.reshape(128, 2048)


def _consts():
    dm = np.zeros((128, 1152), np.float32)
    p = np.arange(128, dtype=np.float32)[:, None]
    j = np.arange(512, dtype=np.float32)[None, :]
    dm[:, 0:512] = p - j
    dm[:, 512:1024] = np.where(p - j <= 0, p - j, NEG)
    for h in range(8):
        for k in range(16):
            dm[:, 1024 + 16 * h + k] = -(2.0 ** (-(h + 1))) * 128.0 * k
    return dm


FUSED = True
_NC_CACHE = {}


def _get_nc(nl):
    if nl not in _NC_CACHE:
        _NC_CACHE[nl] = build(nl)
    return _NC_CACHE[nl]


def kernel(**inputs):
    inp = {k: np.asarray(v) for k, v in inputs.items()}
    x = np.asarray(inp["x"], np.float32)
    nb = x.shape[0]
    dm = _consts()
    packs = [_pack_layer(inp, l) for l in range(DEPTH)]
    xTs = [np.ascontiguousarray(x[b].T) for b in range(nb)]
    f = lambda a: np.ascontiguousarray(np.asarray(a, np.float32))
    if FUSED:
        nc = _get_nc(DEPTH)
        vec_all = np.stack([p[0] for p in packs])
        bd_all = np.stack([p[1] for p in packs])
        shared = {"w_in": f(inp["w_in"]), "w_out": f(inp["w_out"]), "w_up": f(inp["w_up"]), "w_down": f(inp["w_down"]),
                  "bd": bd_all, "vecs": vec_all, "dmat": dm}
        in_maps = [dict(shared, xT=xTs[b]) for b in range(nb)]
        res = run_bass_kernel_spmd(nc, in_maps, core_ids=list(range(nb)))
        outs = [res.results[b]["y"] for b in range(nb)]
    else:
        nc = _get_nc(1)
        cur = xTs
        for l in range(DEPTH):
            shared = {"w_in": f(inp["w_in"][l:l + 1]), "w_out": f(inp["w_out"][l:l + 1]), "w_up": f(inp["w_up"][l:l + 1]),
                      "w_down": f(inp["w_down"][l:l + 1]), "bd": packs[l][1][None], "vecs": packs[l][0][None], "dmat": dm}
            in_maps = [dict(shared, xT=cur[b]) for b in range(nb)]
            res = run_bass_kernel_spmd(nc, in_maps, core_ids=list(range(nb)))
            cur = [np.ascontiguousarray(res.results[b]["y"]) for b in range(nb)]
        outs = cur
    return np.stack([np.ascontiguousarray(o.T) for o in outs]).astype(np.float32)
```

```python
import contextlib
import math
import numpy as np
import concourse.bass as bass
import concourse.mybir as mybir
from concourse.bass_utils import run_bass_kernel_spmd

F32 = mybir.dt.float32
BF16 = mybir.dt.bfloat16
AF = mybir.ActivationFunctionType
ALU = mybir.AluOpType
AX = mybir.AxisListType

D = 2048
S = 2048
DEPTH = 4
DFF = 5632
INW = 5120
TT = 512
NT = S // TT
ALPHA = (2.0 * DEPTH) ** 0.25
NEG = -1.0e6
ENGS = ("pe", "act", "dve", "pool", "sp")

V_CW = 0
V_CB = 32
V_BA = 40
V_BX = 48
V_LAM = 56
V_G1 = 64
V_B1 = 80
V_G2 = 96
V_B2 = 112
V_FCW = 128
V_FCB = 392
V_SUBG = 480
V_LAMV = 481
V_NLI = 737
V_SUBK = 738
V_SUBE = 739
V_LNEPS = 740
V_ONE = 741
NVL = 742
DR_HBA = 0
DR_HBX = 8
DR_CLH = 16
DR_CL = 24
DR_T0 = 32
DR_NEGLAM = 48
DR_S = 50
DR_P = 64
NDER = 128


class _Op:
    __slots__ = ("eng", "fn", "reads", "writes", "dma", "deps", "signal", "tok", "idx", "epoch")


class Prog:
    def __init__(self, nc):
        self.nc = nc
        self.ops = []
        self.last_w = {}
        self.readers = {}
        self.dma_cnt = {}
        self.out_dmas = []
        self.epoch = 0

    def add(self, eng, fn, reads=(), writes=(), dma=None, final=False):
        op = _Op()
        op.eng, op.fn, op.reads, op.writes, op.dma = eng, fn, tuple(reads), tuple(writes), dma
        op.signal, op.tok = False, None
        op.epoch = self.epoch
        op.idx = len(self.ops)
        deps = {}
        for r in op.reads:
            w = self.last_w.get(r)
            if w is not None:
                deps[w.idx] = w
        for k in op.writes:
            w = self.last_w.get(k)
            if w is not None:
                deps[w.idx] = w
            for rd in self.readers.get(k, {}).values():
                deps[rd.idx] = rd
        rk = ("dma", op.idx) if dma is not None else eng
        for r in op.reads:
            self.readers.setdefault(r, {})[rk] = op
        for k in op.writes:
            self.readers[k] = {}
            self.last_w[k] = op
        latest = {}
        for i, dop in deps.items():
            if dop.dma is None:
                if dop.eng not in latest or latest[dop.eng].idx < i:
                    latest[dop.eng] = dop
        op.deps = tuple(sorted([dop for dop in deps.values() if dop.dma is not None] + list(latest.values()), key=lambda o: o.idx))
        if dma is not None:
            n = self.dma_cnt.get(dma, 0) + 1
            self.dma_cnt[dma] = n
            op.tok = (("dma", dma), 16 * n)
            if final:
                self.out_dmas.append(op)
        self.ops.append(op)
        return op

    def emit(self):
        nc = self.nc
        for op in self.ops:
            for d in op.deps:
                if d.dma is None:
                    if d.eng == "pe" and op.eng == "pe" and op.dma is None:
                        continue
                    d.signal = True
        cnt = {}
        for op in self.ops:
            if op.dma is None and op.signal:
                k = ("eng", op.eng, op.epoch)
                cnt[k] = cnt.get(k, 0) + 1
                op.tok = (k, cnt[k])
        with contextlib.ExitStack() as st:
            sems = {}
            for k in cnt:
                sems[k] = st.enter_context(nc.semaphore("s_%s_%d" % (k[1], k[2])))
            for i, k in enumerate(self.dma_cnt):
                sems[("dma", k)] = st.enter_context(nc.semaphore("d_%d" % i))
            block = st.enter_context(nc.Block())
            by_eng = {e: [o for o in self.ops if o.eng == e] for e in ENGS}
            out_dmas = self.out_dmas

            def run(eng_name, engine):
                seen = {}
                for op in by_eng[eng_name]:
                    for d in op.deps:
                        if d.tok is None:
                            continue
                        if d.dma is None and d.eng == "pe" and eng_name == "pe" and op.dma is None:
                            continue
                        k, v = d.tok
                        if seen.get(k, 0) >= v:
                            continue
                        seen[k] = v
                        engine.wait_ge(sems[k], v)
                    ins = op.fn(engine)
                    if op.dma is not None:
                        ins.then_inc(sems[op.tok[0]], 16)
                    elif op.signal:
                        ins.then_inc(sems[op.tok[0]], 1)
                if eng_name == "sp":
                    for op in out_dmas:
                        k, v = op.tok
                        if seen.get(k, 0) >= v:
                            continue
                        seen[k] = v
                        engine.wait_ge(sems[k], v)

            @block.tensor
            def _(e):
                run("pe", e)

            @block.scalar
            def _(e):
                run("act", e)

            @block.vector
            def _(e):
                run("dve", e)

            @block.gpsimd
            def _(e):
                run("pool", e)

            @block.sync
            def _(e):
                run("sp", e)


def build(nl, dbg=99, ntiles=NT):
    nc = bass.Bass("TRN2", target_bir_lowering=False)

    def dram(name, shape, kind):
        return nc.dram_tensor(name, shape, F32, kind=kind).ap()

    xT = dram("xT", [D, S], "ExternalInput")
    w_in = dram("w_in", [nl, D, INW], "ExternalInput")
    w_out = dram("w_out", [nl, D, D], "ExternalInput")
    w_up = dram("w_up", [nl, D, 2 * DFF], "ExternalInput")
    w_down = dram("w_down", [nl, DFF, D], "ExternalInput")
    bd = dram("bd", [nl, 128, 2048], "ExternalInput")
    vecs = dram("vecs", [nl, 128, NVL], "ExternalInput")
    dmat = dram("dmat", [128, 1152], "ExternalInput")
    y = dram("y", [D, S], "ExternalOutput")
    xs = [dram("xs0", [D, S], "Internal"), dram("xs1", [D, S], "Internal")] if nl > 1 else []

    with contextlib.ExitStack() as st:
        def sb(name, shape, dt):
            return st.enter_context(nc.sbuf_tensor(name, shape, dt))

        KT = sb("KT", [128, 8, S], BF16)
        VC = sb("VC", [128, 16, 1024], BF16)
        SL = [sb("SL0", [128, 8192], BF16), sb("SL1", [128, 8192], BF16)]
        XB = sb("XB", [128, 16, TT], BF16)
        R = sb("R", [128, 16, TT], F32)
        CC = sb("CC", [128, 16, TT], BF16)
        QH = sb("QH", [128, 8, TT], BF16)
        DM = sb("DM", [128, 1152], F32)
        ONES_LN = sb("ONES_LN", [128, 128], F32)
        ONES_SUB = sb("ONES_SUB", [128, 128], F32)
        ONES_B = sb("ONES_B", [128, 128], BF16)
        BD = sb("BD", [128, 2048], BF16)
        VEC = sb("VEC", [128, NVL], F32)
        DER = sb("DER", [128, NDER], F32)
        HRG = sb("HRG", [128, 8, 3], F32)
        HST = sb("HST", [128, 8], F32)
        HF = sb("HF", [128, 88, 2], F32)
        NF = 9
        F = [sb("F%d" % i, [128, 516], F32) for i in range(NF)]
        B = [sb("B%d" % i, [128, TT], BF16) for i in range(4)]
        QZ = [sb("QZ%d" % i, [128, TT], BF16) for i in range(2)]
        PS = [st.enter_context(nc.psum_tensor("PS%d" % i, [128, TT], F32)) for i in range(8)]

        P = Prog(nc)
        A = P.add
        SL16 = [s[:].rearrange("p (k n) -> p k n", k=16) for s in SL] + [CC[:]]
        SL4 = [s[:].rearrange("p (k n) -> p k n", k=4) for s in SL] + [CC[:].rearrange("p (a b) n -> p a (b n)", a=4)]
        slab_n = [0]
        slab_rot = [2]
        slab_keys = [[("SL", 0)], [("SL", 1)], [("CC", m) for m in range(16)]]

        def fk(i):
            return ("F", i)

        def vcol(c, n=1):
            return VEC[:, c:c + n]

        def load_slab16(src):
            b = slab_n[0] % slab_rot[0]
            slab_n[0] += 1
            A("pool", lambda e: e.dma_start(out=SL16[b], in_=src.rearrange("(k p) n -> p k n", p=128)),
              writes=slab_keys[b], dma=("SL", b))
            return b

        def load_slab4(src):
            b = slab_n[0] % slab_rot[0]
            slab_n[0] += 1
            A("pool", lambda e: e.dma_start(out=SL4[b], in_=src.rearrange("(k p) n -> p k n", p=128)),
              writes=slab_keys[b], dma=("SL", b))
            return b

        A("sp", lambda e: e.dma_start(out=DM[:], in_=dmat), writes=["DM"], dma="dm")
        A("dve", lambda e: e.memset(ONES_LN[:], 1.0 / D), writes=["ONES"])
        A("dve", lambda e: e.memset(ONES_SUB[:], 1.0 / 128), writes=["ONES"])
        A("dve", lambda e: e.memset(ONES_B[:], 1.0), writes=["ONES"])
        for qi in range(2):
            A("dve", lambda e, qi=qi: e.memset(QZ[qi][:], 0.0), writes=[("QZ", qi)])
        D0 = DM[:, 0:512]
        DMK = DM[:, 512:1024]

        def cbias(h, k):
            c = 1024 + h * 16 + k
            return DM[:, c:c + 1]

        evac_rr = [0]

        def evac_copy(out_ap, ps_i, wkeys):
            evac_rr[0] ^= 1
            if evac_rr[0]:
                A("act", lambda e: e.activation(out=out_ap, in_=PS[ps_i][:], func=AF.Copy),
                  reads=[("PS", ps_i)], writes=wkeys)
            else:
                A("dve", lambda e: e.tensor_copy(out=out_ap, in_=PS[ps_i][:]),
                  reads=[("PS", ps_i)], writes=wkeys)

        def proj_fm(b, j, ps_i, rhs_of_kc, rkeys, nk=16):
            for kc in range(nk):
                A("pe", lambda e, kc=kc: e.matmul(PS[ps_i][:], lhsT=SL16[b][:, kc, j * 128:(j + 1) * 128],
                                                  rhs=rhs_of_kc(kc), start=(kc == 0), stop=(kc == nk - 1)),
                  reads=slab_keys[b] + rkeys, writes=[("PS", ps_i)])

        xb_keys = [("XB", m) for m in range(16)]
        cc_keys = [("CC", m) for m in range(16)]

        def layer_norm(gcol, bcol, write_xb):
            pm, pv = 6, 7
            for m in range(16):
                A("pe", lambda e, m=m: e.matmul(PS[pm][:], lhsT=ONES_LN[:], rhs=R[:, m, :], start=(m == 0), stop=(m == 15)),
                  reads=["ONES", ("R", m)], writes=[("PS", pm)])
            for m in range(16):
                A("dve", lambda e, m=m: e.tensor_tensor(out=R[:, m, :], in0=R[:, m, :], in1=PS[pm][:], op=ALU.subtract),
                  reads=[("R", m), ("PS", pm)], writes=[("R", m)])
                sq = m % 2
                A("act", lambda e, m=m, sq=sq: e.activation(out=F[sq][:, 0:TT], in_=R[:, m, :], func=AF.Square),
                  reads=[("R", m)], writes=[fk(sq)])
                A("pe", lambda e, m=m, sq=sq: e.matmul(PS[pv][:], lhsT=ONES_LN[:], rhs=F[sq][:, 0:TT], start=(m == 0), stop=(m == 15)),
                  reads=["ONES", fk(sq)], writes=[("PS", pv)])
            A("act", lambda e: e.activation(out=F[2][:, 0:TT], in_=PS[pv][:], func=AF.Ln, bias=vcol(V_LNEPS)),
              reads=[("PS", pv), "vecs"], writes=[fk(2)])
            A("act", lambda e: e.activation(out=F[3][:, 0:TT], in_=F[2][:, 0:TT], func=AF.Exp, scale=-0.5), reads=[fk(2)], writes=[fk(3)])
            for m in range(16):
                A("dve", lambda e, m=m: e.tensor_tensor(out=R[:, m, :], in0=R[:, m, :], in1=F[3][:, 0:TT], op=ALU.mult),
                  reads=[("R", m), fk(3)], writes=[("R", m)])
                if write_xb:
                    A("act", lambda e, m=m: e.activation(out=XB[:, m, :], in_=R[:, m, :], func=AF.Identity,
                                                         scale=vcol(gcol + m), bias=vcol(bcol + m)),
                      reads=[("R", m), "vecs"], writes=[("XB", m)])
                A("act", lambda e, m=m: e.activation(out=R[:, m, :], in_=R[:, m, :], func=AF.Identity,
                                                     scale=vcol(gcol + m), bias=vcol(bcol + m)),
                  reads=[("R", m), "vecs"], writes=[("R", m)])

        for l in range(nl):
            src = xT if l == 0 else xs[(l - 1) % 2]
            srcn = "xT" if l == 0 else "xs%d" % ((l - 1) % 2)
            dst = y if l == nl - 1 else xs[l % 2]
            dstn = "y" if l == nl - 1 else "xs%d" % (l % 2)
            A("sp", lambda e, l=l: e.dma_start(out=VEC[:], in_=vecs[l]), writes=["vecs"], dma="vecs")
            A("pool", lambda e, l=l: e.dma_start(out=BD[:], in_=bd[l]), writes=["BD"], dma="BD")
            A("dve", lambda e: e.memset(HRG[:], 0.0), writes=["HRG"])
            A("dve", lambda e: e.memset(HST[:], 0.0), writes=["HST"])
            A("dve", lambda e: e.memset(HF[:], 0.0), writes=["HF"])
            A("dve", lambda e: e.tensor_scalar(out=DER[:, DR_HBA:DR_HBA + 16], in0=VEC[:, V_BA:V_BA + 16], scalar1=0.5,
                                               scalar2=None, op0=ALU.mult), reads=["vecs"], writes=["DER"])
            A("act", lambda e: e.activation(out=DER[:, DR_T0:DR_T0 + 8], in_=VEC[:, V_LAM:V_LAM + 8], func=AF.Exp, scale=-1.0),
              reads=["vecs", "DER"], writes=["DER"])
            A("act", lambda e: e.activation(out=DER[:, DR_T0:DR_T0 + 8], in_=DER[:, DR_T0:DR_T0 + 8], func=AF.Ln, bias=vcol(V_ONE)),
              reads=["vecs", "DER"], writes=["DER"])
            A("dve", lambda e: e.tensor_scalar(out=DER[:, DR_CLH:DR_CLH + 8], in0=DER[:, DR_T0:DR_T0 + 8], scalar1=-4.0,
                                               scalar2=None, op0=ALU.mult), reads=["DER"], writes=["DER"])
            A("dve", lambda e: e.tensor_scalar(out=DER[:, DR_CL:DR_CL + 8], in0=DER[:, DR_T0:DR_T0 + 8], scalar1=-8.0,
                                               scalar2=None, op0=ALU.mult), reads=["DER"], writes=["DER"])
            for i in range(2):
                A("dve", lambda e, i=i: e.tensor_tensor(out=DER[:, DR_P:DR_P + 64], in0=VEC[:, V_LAMV + 128 * i:V_LAMV + 128 * i + 64],
                                                        in1=VEC[:, V_LAMV + 128 * i + 64:V_LAMV + 128 * i + 128], op=ALU.mult),
                  reads=["vecs", "DER"], writes=["DER"])
                A("dve", lambda e, i=i: e.tensor_reduce(out=DER[:, DR_S + i:DR_S + i + 1], in_=DER[:, DR_P:DR_P + 64], axis=AX.X, op=ALU.add),
                  reads=["DER"], writes=["DER"])
            A("act", lambda e: e.activation(out=DER[:, DR_S:DR_S + 2], in_=DER[:, DR_S:DR_S + 2], func=AF.Exp),
              reads=["DER"], writes=["DER"])
            A("dve", lambda e: e.scalar_tensor_tensor(out=DER[:, DR_NEGLAM:DR_NEGLAM + 1], in0=DER[:, DR_S + 1:DR_S + 2],
                                                      scalar=vcol(V_NLI), in1=DER[:, DR_S:DR_S + 1], op0=ALU.add, op1=ALU.subtract),
              reads=["DER", "vecs"], writes=["DER"])

            for tt in range(ntiles):
                t0 = tt * TT
                P.epoch += 1
                xsrc = src[:, t0:t0 + TT].rearrange("(m p) n -> p m n", p=128)
                A("sp", lambda e, xsrc=xsrc: e.dma_start(out=R[:], in_=xsrc), reads=[("X", srcn, tt)],
                  writes=[("R", m) for m in range(16)], dma="xld")
                A("pool", lambda e, xsrc=xsrc: e.dma_start(out=XB[:], in_=xsrc), reads=[("X", srcn, tt)],
                  writes=xb_keys, dma="xbld")
                xrhs = lambda kc: XB[:, kc, :]

                for sg in range(2 if dbg >= 1 else 0):
                    b = load_slab16(w_in[l][:, 1024 + 512 * sg:1024 + 512 * (sg + 1)])
                    for j in range(4):
                        c = 4 * sg + j
                        proj_fm(b, j, c, xrhs, xb_keys)
                        A("act", lambda e, c=c: e.activation(out=QH[:, c, :], in_=PS[c][:], func=AF.Gelu_apprx_tanh),
                          reads=[("PS", c)], writes=[("QH", c)])
                fillers = []
                fstate = {}

                def mk_filler(u):
                    grp, jj = u // 4, u % 4
                    def run():
                        pi = 6 + (u % 2)
                        if jj == 0:
                            col = {0: 3072, 1: 3584, 2: 4096, 3: 4608, 4: 2048, 5: 2560}[grp]
                            fstate[grp] = load_slab16(w_in[l][:, col:col + 512])
                        b = fstate[grp]
                        if grp < 2:
                            h = 4 * grp + jj
                            proj_fm(b, jj, pi, xrhs, xb_keys)
                            evac_copy(KT[:, h, t0:t0 + TT], pi, [("KT", h, tt)])
                        elif grp < 4:
                            sv, ts = grp - 2, jj
                            for kc in range(16):
                                A("pe", lambda e, kc=kc: e.matmul(
                                    PS[pi][:], lhsT=XB[:, kc, ts * 128:(ts + 1) * 128], rhs=SL16[b][:, kc, :],
                                    start=(kc == 0), stop=(kc == 15)),
                                  reads=slab_keys[b] + [("XB", kc)], writes=[("PS", pi)])
                            evac_copy(VC[:, 4 * tt + ts, sv * 512:(sv + 1) * 512], pi, [("VC", 4 * tt + ts, sv)])
                        else:
                            h = 4 * (grp - 4) + jj
                            proj_fm(b, jj, pi, xrhs, xb_keys)
                            evac_copy(QH[:, h, :], pi, [("QH", h)])
                    return run

                if dbg >= 3:
                    fillers = [mk_filler(u) for u in range(24)]
                for sx in range(2 if dbg >= 2 else 0):
                    b = load_slab16(w_in[l][:, 512 * sx:512 * (sx + 1)])
                    for j in range(4):
                        proj_fm(b, j, j, xrhs, xb_keys)
                    for j in range(4):
                        c = 4 * sx + j
                        xh = c % 2
                        ub = c % 2
                        pa, px = 4, 5
                        A("act", lambda e, j=j, xh=xh: e.activation(out=F[xh][:, 3:515], in_=PS[j][:], func=AF.Copy),
                          reads=[("PS", j)], writes=[fk(xh)])
                        A("dve", lambda e, c=c, xh=xh: e.tensor_copy(out=F[xh][:, 0:3], in_=HRG[:, c, :]),
                          reads=["HRG"], writes=[fk(xh)])
                        A("act", lambda e, c=c, j=j: e.activation(out=F[2][:, 0:TT], in_=PS[j][:], func=AF.Identity,
                                                                  scale=vcol(V_CW + 3 * 8 + c), bias=vcol(V_CB + c)),
                          reads=[("PS", j), "vecs"], writes=[fk(2)])
                        for k in range(3):
                            A("dve", lambda e, c=c, xh=xh, k=k: e.scalar_tensor_tensor(
                                out=F[2][:, 0:TT], in0=F[xh][:, k:k + TT], scalar=vcol(V_CW + k * 8 + c), in1=F[2][:, 0:TT],
                                op0=ALU.mult, op1=ALU.add), reads=[fk(xh), fk(2), "vecs"], writes=[fk(2)])
                        A("dve", lambda e, c=c, xh=xh: e.tensor_copy(out=HRG[:, c, :], in_=F[xh][:, 512:515]),
                          reads=[fk(xh)], writes=["HRG"])
                        A("act", lambda e, ub=ub: e.activation(out=B[ub][:], in_=F[2][:, 0:TT], func=AF.Copy),
                          reads=[fk(2)], writes=[("B", ub)])
                        if fillers:
                            fillers.pop(0)()
                        A("pe", lambda e, c=c, ub=ub, pa=pa: e.matmul(PS[pa][:], lhsT=BD[:, c * 128:(c + 1) * 128], rhs=B[ub][:],
                                                                     start=True, stop=True),
                          reads=["BD", ("B", ub)], writes=[("PS", pa)])
                        A("pe", lambda e, c=c, ub=ub, px=px: e.matmul(PS[px][:], lhsT=BD[:, (8 + c) * 128:(9 + c) * 128], rhs=B[ub][:],
                                                                     start=True, stop=True),
                          reads=["BD", ("B", ub)], writes=[("PS", px)])
                        A("act", lambda e, c=c, pa=pa: e.activation(out=F[3][:, 0:TT], in_=PS[pa][:], func=AF.Tanh, scale=0.5,
                                                                    bias=DER[:, DR_HBA + c:DR_HBA + c + 1]),
                          reads=[("PS", pa), "DER"], writes=[fk(3)])
                        A("act", lambda e, c=c, px=px: e.activation(out=F[4][:, 0:TT], in_=PS[px][:], func=AF.Tanh, scale=0.5,
                                                                    bias=DER[:, DR_HBX + c:DR_HBX + c + 1]),
                          reads=[("PS", px), "DER"], writes=[fk(4)])
                        A("act", lambda e, c=c: e.activation(out=F[5][:, 0:TT], in_=F[3][:, 0:TT], func=AF.Exp,
                                                             scale=DER[:, DR_CLH + c:DR_CLH + c + 1], bias=DER[:, DR_CLH + c:DR_CLH + c + 1]),
                          reads=[fk(3), "DER"], writes=[fk(5)])
                        A("act", lambda e, c=c: e.activation(out=F[6][:, 0:TT], in_=F[3][:, 0:TT], func=AF.Exp,
                                                             scale=DER[:, DR_CL + c:DR_CL + c + 1], bias=DER[:, DR_CL + c:DR_CL + c + 1]),
                          reads=[fk(3), "DER"], writes=[fk(6)])
                        A("act", lambda e: e.activation(out=F[6][:, 0:TT], in_=F[6][:, 0:TT], func=AF.Sqrt, scale=-1.0, bias=vcol(V_ONE)),
                          reads=[fk(6), "vecs"], writes=[fk(6)])
                        A("dve", lambda e: e.scalar_tensor_tensor(out=F[7][:, 0:TT], in0=F[4][:, 0:TT], scalar=1.0, in1=F[2][:, 0:TT],
                                                                  op0=ALU.add, op1=ALU.mult), reads=[fk(4), fk(2)], writes=[fk(7)])
                        A("dve", lambda e: e.scalar_tensor_tensor(out=F[7][:, 0:TT], in0=F[7][:, 0:TT], scalar=0.5, in1=F[6][:, 0:TT],
                                                                  op0=ALU.mult, op1=ALU.mult), reads=[fk(7), fk(6)], writes=[fk(7)])
                        A("dve", lambda e, c=c: e.tensor_tensor_scan(out=F[8][:, 0:TT], data0=F[5][:, 0:TT], data1=F[7][:, 0:TT],
                                                                     initial=HST[:, c:c + 1], op0=ALU.mult, op1=ALU.add),
                          reads=[fk(5), fk(7), "HST"], writes=[fk(8)])
                        A("dve", lambda e, c=c: e.tensor_copy(out=HST[:, c:c + 1], in_=F[8][:, TT - 1:TT]),
                          reads=[fk(8)], writes=["HST"])
                        A("dve", lambda e, c=c: e.tensor_tensor(out=CC[:, c, :], in0=F[8][:, 0:TT], in1=QH[:, c, :], op=ALU.mult),
                          reads=[fk(8), ("QH", c)], writes=[("CC", c)])
                        for _ in range(2):
                            if fillers:
                                fillers.pop(0)()
                while fillers:
                    fillers.pop(0)()
                nkt = 4 * tt + 4
                steps = [(h, c, kt) for h in range(8 if dbg >= 4 else 0) for c in range(2) for kt in range(nkt)]
                LA = 2
                SBK = (0, 1, 7)
                deferred = []

                def qz_copy(h, c):
                    if c == 0:
                        A("act", lambda e: e.activation(out=QZ[0][0:64, :], in_=QH[0:64, h, :], func=AF.Copy),
                          reads=[("QH", h)], writes=[("QZ", 0)])
                    else:
                        A("dve", lambda e: e.tensor_copy(out=QZ[1][64:128, :], in_=QH[64:128, h, :]),
                          reads=[("QH", h)], writes=[("QZ", 1)])

                def emit_qk(i):
                    h, c, kt = steps[i]
                    slope = 2.0 ** (-(h + 1))
                    r = kt - 4 * tt
                    n0 = 128 * r if r > 0 else 0
                    N = TT - n0
                    psi = SBK[i % 3]
                    ti = i % 2
                    bi = i % 4
                    qi = c
                    if kt == 0:
                        g = 2 * h + c
                        for gg in ([0, 1] if g == 0 else [g + 1]):
                            if gg < 16:
                                qz_copy(gg // 2, gg % 2)
                    A("pe", lambda e, h=h, kt=kt, n0=n0, N=N, psi=psi, qi=qi: e.matmul(
                        PS[psi][:, 0:N], lhsT=KT[:, h, kt * 128:(kt + 1) * 128], rhs=QZ[qi][:, n0:TT], start=True, stop=True),
                      reads=[("KT", h, kt // 4), ("QZ", qi)], writes=[("PS", psi)])
                    dsrc = DMK if r >= 0 else D0
                    A("dve", lambda e, N=N, psi=psi, ti=ti, dsrc=dsrc, slope=slope: e.scalar_tensor_tensor(
                        out=F[ti][:, 0:N], in0=dsrc[:, 0:N], scalar=8.0 * slope, in1=PS[psi][:, 0:N],
                        op0=ALU.mult, op1=ALU.add), reads=["DM", ("PS", psi)], writes=[fk(ti)])
                    if r < 0:
                        A("act", lambda e, N=N, ti=ti, bi=bi, h=h, r=r: e.activation(
                            out=B[bi][:, 0:N], in_=F[ti][:, 0:N], func=AF.Exp, scale=0.125, bias=cbias(h, -r)),
                          reads=[fk(ti), "DM"], writes=[("B", bi)])
                    else:
                        A("act", lambda e, N=N, ti=ti, bi=bi: e.activation(
                            out=B[bi][:, 0:N], in_=F[ti][:, 0:N], func=AF.Exp, scale=0.125),
                          reads=[fk(ti)], writes=[("B", bi)])

                def head_tail(h):
                    A("pe", lambda e: e.matmul(PS[6][:], lhsT=ONES_SUB[:], rhs=F[6][:, 0:TT], start=True, stop=True),
                      reads=["ONES", fk(6)], writes=[("PS", 6)])
                    A("act", lambda e: e.activation(out=F[7][:, 0:TT], in_=PS[6][:], func=AF.Ln, scale=vcol(V_SUBK), bias=vcol(V_SUBE)),
                      reads=[("PS", 6), "vecs"], writes=[fk(7)])
                    A("act", lambda e: e.activation(out=F[7][:, 0:TT], in_=F[7][:, 0:TT], func=AF.Exp, scale=-0.5), reads=[fk(7)], writes=[fk(7)])
                    A("dve", lambda e, h=h: e.scalar_tensor_tensor(out=CC[:, 8 + h, :], in0=F[5][:, 0:TT], scalar=vcol(V_SUBG),
                                                                  in1=F[7][:, 0:TT], op0=ALU.mult, op1=ALU.mult),
                      reads=[fk(5), fk(7), "vecs"], writes=[("CC", 8 + h)])

                def emit_pv(i):
                    h, c, kt = steps[i]
                    r = kt - 4 * tt
                    n0 = 128 * r if r > 0 else 0
                    N = TT - n0
                    bi = i % 4
                    po, pl = 2 + 2 * c, 3 + 2 * c
                    A("pe", lambda e, h=h, kt=kt, n0=n0, N=N, bi=bi, po=po: e.matmul(
                        PS[po][:, n0:TT], lhsT=VC[:, kt, h * 128:(h + 1) * 128], rhs=B[bi][:, 0:N],
                        start=(kt == 0), stop=(kt == nkt - 1)),
                      reads=[("VC", kt, h // 4), ("B", bi)], writes=[("PS", po)])
                    A("pe", lambda e, n0=n0, N=N, bi=bi, pl=pl, kt=kt: e.matmul(
                        PS[pl][:, n0:TT], lhsT=ONES_B[:], rhs=B[bi][:, 0:N],
                        start=(kt == 0), stop=(kt == nkt - 1)),
                      reads=["ONES", ("B", bi)], writes=[("PS", pl)])
                    if kt == nkt - 1:
                        A("act", lambda e, pl=pl: e.activation(out=F[2][:, 0:TT], in_=PS[pl][:], func=AF.Ln),
                          reads=[("PS", pl)], writes=[fk(2)])
                        A("act", lambda e: e.activation(out=F[2][:, 0:TT], in_=F[2][:, 0:TT], func=AF.Exp, scale=-1.0),
                          reads=[fk(2)], writes=[fk(2)])
                        A("dve", lambda e, po=po, c=c: e.tensor_tensor(out=F[3 + c][:, 0:TT], in0=PS[po][:], in1=F[2][:, 0:TT], op=ALU.mult),
                          reads=[("PS", po), fk(2)], writes=[fk(3 + c)])
                        if c == 1:
                            A("dve", lambda e: e.scalar_tensor_tensor(out=F[5][:, 0:TT], in0=F[4][:, 0:TT], scalar=DER[:, DR_NEGLAM:DR_NEGLAM + 1],
                                                                      in1=F[3][:, 0:TT], op0=ALU.mult, op1=ALU.add),
                              reads=[fk(3), fk(4), "DER"], writes=[fk(5)])
                            A("act", lambda e: e.activation(out=F[6][:, 0:TT], in_=F[5][:, 0:TT], func=AF.Square), reads=[fk(5)], writes=[fk(6)])
                            deferred.append((i + 4, h))

                for i in range(len(steps) + LA):
                    if i < len(steps):
                        emit_qk(i)
                    if i >= LA:
                        emit_pv(i - LA)
                    while deferred and deferred[0][0] <= i - LA:
                        head_tail(deferred.pop(0)[1])
                while deferred:
                    head_tail(deferred.pop(0)[1])
                for so in range(4 if dbg >= 5 else 0):
                    b = load_slab16(w_out[l][:, 512 * so:512 * (so + 1)])
                    for j in range(4):
                        m = 4 * so + j
                        pi = m % 8
                        proj_fm(b, j, pi, lambda kc: CC[:, kc, :], cc_keys)
                        A("dve", lambda e, m=m, pi=pi: e.scalar_tensor_tensor(out=R[:, m, :], in0=R[:, m, :], scalar=ALPHA, in1=PS[pi][:],
                                                                            op0=ALU.mult, op1=ALU.add),
                          reads=[("R", m), ("PS", pi)], writes=[("R", m)])
                if dbg >= 6:
                    layer_norm(V_G1, V_B1, True)
                HB = lambda hb, j: QH[:, 4 * hb + j, :]

                def down_proj(s, bdn, part=None):
                    hb = s % 2
                    for m in (range(16) if part is None else range(4 * part, 4 * part + 4)):
                        pi = 4 + (m % 4)
                        for kc in range(4):
                            A("pe", lambda e, m=m, kc=kc, pi=pi, hb=hb: e.matmul(
                                PS[pi][:], lhsT=SL4[bdn][:, kc, m * 128:(m + 1) * 128], rhs=HB(hb, kc),
                                start=(kc == 0), stop=(kc == 3)),
                              reads=slab_keys[bdn] + [("QH", 4 * hb + kc)], writes=[("PS", pi)])
                        if s == 0:
                            A("dve", lambda e, m=m, pi=pi: e.scalar_tensor_tensor(out=R[:, m, :], in0=R[:, m, :], scalar=ALPHA, in1=PS[pi][:],
                                                                                op0=ALU.mult, op1=ALU.add),
                              reads=[("R", m), ("PS", pi)], writes=[("R", m)])
                        else:
                            A("dve", lambda e, m=m, pi=pi: e.tensor_tensor(out=R[:, m, :], in0=R[:, m, :], in1=PS[pi][:], op=ALU.add),
                              reads=[("R", m), ("PS", pi)], writes=[("R", m)])

                slab_rot[0] = 3
                for s in range(11 if dbg >= 7 else 0):
                    hb = s % 2
                    for half in range(2):
                        b = load_slab16(w_up[l][:, half * DFF + 512 * s:half * DFF + 512 * (s + 1)])
                        for j in range(4):
                            proj_fm(b, j, j, xrhs, xb_keys)
                        for j in range(4):
                            ch = half * 44 + 4 * s + j
                            uh = 4 + (j % 2)
                            acc = j if half == 0 else 6 + (j % 2)
                            A("act", lambda e, j=j, uh=uh: e.activation(out=F[uh][:, 2:514], in_=PS[j][:], func=AF.Copy),
                              reads=[("PS", j)], writes=[fk(uh)])
                            A("dve", lambda e, ch=ch, uh=uh: e.tensor_copy(out=F[uh][:, 0:2], in_=HF[:, ch, :]),
                              reads=["HF"], writes=[fk(uh)])
                            A("act", lambda e, ch=ch, j=j, acc=acc: e.activation(
                                out=F[acc][:, 0:TT], in_=PS[j][:], func=AF.Identity,
                                scale=vcol(V_FCW + 2 * 88 + ch), bias=vcol(V_FCB + ch)),
                              reads=[("PS", j), "vecs"], writes=[fk(acc)])
                            for k in range(2):
                                A("dve", lambda e, ch=ch, uh=uh, acc=acc, k=k: e.scalar_tensor_tensor(
                                    out=F[acc][:, 0:TT], in0=F[uh][:, k:k + TT], scalar=vcol(V_FCW + k * 88 + ch), in1=F[acc][:, 0:TT],
                                    op0=ALU.mult, op1=ALU.add), reads=[fk(uh), fk(acc), "vecs"], writes=[fk(acc)])
                            A("dve", lambda e, ch=ch, uh=uh: e.tensor_copy(out=HF[:, ch, :], in_=F[uh][:, 512:514]),
                              reads=[fk(uh)], writes=["HF"])
                            if half == 0:
                                A("act", lambda e, acc=acc: e.activation(out=F[acc][:, 0:TT], in_=F[acc][:, 0:TT], func=AF.Gelu_apprx_tanh),
                                  reads=[fk(acc)], writes=[fk(acc)])
                            else:
                                A("dve", lambda e, j=j, acc=acc, hb=hb: e.tensor_tensor(out=HB(hb, j), in0=F[j][:, 0:TT], in1=F[acc][:, 0:TT], op=ALU.mult),
                                  reads=[fk(j), fk(acc)], writes=[("QH", 4 * hb + j)])
                                if s > 0:
                                    if j == 0:
                                        bdn_cur = load_slab4(w_down[l][512 * (s - 1):512 * s, :])
                                    down_proj(s - 1, bdn_cur, part=j)
                if dbg >= 7:
                    bdn = load_slab4(w_down[l][512 * 10:512 * 11, :])
                    down_proj(10, bdn)
                slab_rot[0] = 2
                if dbg >= 8:
                    layer_norm(V_G2, V_B2, False)
                xdst = dst[:, t0:t0 + TT].rearrange("(m p) n -> p m n", p=128)
                A("sp", lambda e, xdst=xdst: e.dma_start(out=xdst, in_=R[:]), reads=[("R", m) for m in range(16)],
                  writes=[("X", dstn, tt)], dma="xst", final=(l == nl - 1))
        P.emit()
    return nc


def _pack_layer(inp, l):
    f = lambda a: np.asarray(a, np.float32)
    v = np.zeros((128, NVL), np.float32)
    pc = lambda a, n: f(a).reshape(n, 128).T
    cw = f(inp["rg_conv_w"][l])
    for k in range(4):
        v[:, V_CW + 8 * k:V_CW + 8 * k + 8] = pc(cw[k], 8)
    v[:, V_CB:V_CB + 8] = pc(inp["rg_conv_b"][l], 8)
    v[:, V_BA:V_BA + 8] = pc(inp["rg_gate_a_b"][l], 8)
    v[:, V_BX:V_BX + 8] = pc(inp["rg_gate_x_b"][l], 8)
    v[:, V_LAM:V_LAM + 8] = pc(inp["rg_lambda"][l], 8)
    v[:, V_G1:V_G1 + 16] = pc(inp["ln_mix_g"][l], 16)
    v[:, V_B1:V_B1 + 16] = pc(inp["ln_mix_b"][l], 16)
    v[:, V_G2:V_G2 + 16] = pc(inp["ln_ffn_g"][l], 16)
    v[:, V_B2:V_B2 + 16] = pc(inp["ln_ffn_b"][l], 16)
    fw = f(inp["ffn_conv_w"][l])
    for k in range(3):
        v[:, V_FCW + 88 * k:V_FCW + 88 * (k + 1)] = pc(fw[k], 88)
    v[:, V_FCB:V_FCB + 88] = pc(inp["ffn_conv_b"][l], 88)
    v[:, V_SUBG] = f(inp["subln_g"][l])
    lamv = np.concatenate([f(inp["lam_q1"][l]), f(inp["lam_k1"][l]), f(inp["lam_q2"][l]), f(inp["lam_k2"][l])])
    v[:, V_LAMV:V_LAMV + 256] = lamv[None, :]
    lam_init = 0.8 - 0.6 * math.exp(-0.3 * l)
    k2 = 1.0 / (1.0 - lam_init) ** 2
    v[:, V_NLI] = -lam_init
    v[:, V_SUBK] = k2
    v[:, V_SUBE] = 1e-5 * k2
    v[:, V_LNEPS] = 1e-5
    v[:, V_ONE] = 1.0
    bdl = np.zeros((128, 2, 8, 128), np.float32)
    for g, name in enumerate(("rg_gate_a_w", "rg_gate_x_w")):
        w = f(inp[name][l])
        for c in range(8):
            bdl[0:64, g, c, 0:64] = w[2 * c]
            bdl[64:128, g, c, 64:128] = w[2 * c + 1]
    return v, bdl.reshape(128, 2048)


def _consts():
    dm = np.zeros((128, 1152), np.float32)
    p = np.arange(128, dtype=np.float32)[:, None]
    j = np.arange(512, dtype=np.float32)[None, :]
    dm[:, 0:512] = p - j
    dm[:, 512:1024] = np.where(p - j <= 0, p - j, NEG)
    for h in range(8):
        for k in range(16):
            dm[:, 1024 + 16 * h + k] = -(2.0 ** (-(h + 1))) * 128.0 * k
    return dm


FUSED = True
_NC_CACHE = {}


def _get_nc(nl):
    if nl not in _NC_CACHE:
        _NC_CACHE[nl] = build(nl)
    return _NC_CACHE[nl]


def kernel(**inputs):
    inp = {k: np.asarray(v) for k, v in inputs.items()}
    x = np.asarray(inp["x"], np.float32)
    nb = x.shape[0]
    dm = _consts()
    packs = [_pack_layer(inp, l) for l in range(DEPTH)]
    xTs = [np.ascontiguousarray(x[b].T) for b in range(nb)]
    f = lambda a: np.ascontiguousarray(np.asarray(a, np.float32))
    if FUSED:
        nc = _get_nc(DEPTH)
        vec_all = np.stack([p[0] for p in packs])
        bd_all = np.stack([p[1] for p in packs])
        shared = {"w_in": f(inp["w_in"]), "w_out": f(inp["w_out"]), "w_up": f(inp["w_up"]), "w_down": f(inp["w_down"]),
                  "bd": bd_all, "vecs": vec_all, "dmat": dm}
        in_maps = [dict(shared, xT=xTs[b]) for b in range(nb)]
        res = run_bass_kernel_spmd(nc, in_maps, core_ids=list(range(nb)))
        outs = [res.results[b]["y"] for b in range(nb)]
    else:
        nc = _get_nc(1)
        cur = xTs
        for l in range(DEPTH):
            shared = {"w_in": f(inp["w_in"][l:l + 1]), "w_out": f(inp["w_out"][l:l + 1]), "w_up": f(inp["w_up"][l:l + 1]),
                      "w_down": f(inp["w_down"][l:l + 1]), "bd": packs[l][1][None], "vecs": packs[l][0][None], "dmat": dm}
            in_maps = [dict(shared, xT=cur[b]) for b in range(nb)]
            res = run_bass_kernel_spmd(nc, in_maps, core_ids=list(range(nb)))
            cur = [np.ascontiguousarray(res.results[b]["y"]) for b in range(nb)]
        outs = cur
    return np.stack([np.ascontiguousarray(o.T) for o in outs]).astype(np.float32)
```

```python
import contextlib
import math
import numpy as np
import concourse.bass as bass
import concourse.mybir as mybir
from concourse.bass_utils import run_bass_kernel_spmd

F32 = mybir.dt.float32
BF16 = mybir.dt.bfloat16
AF = mybir.ActivationFunctionType
ALU = mybir.AluOpType
AX = mybir.AxisListType

D = 2048
S = 2048
DEPTH = 4
DFF = 5632
INW = 5120
TT = 512
NT = S // TT
ALPHA = (2.0 * DEPTH) ** 0.25
NEG = -1.0e6
ENGS = ("pe", "act", "dve", "pool", "sp")

V_CW = 0
V_CB = 32
V_BA = 40
V_BX = 48
V_LAM = 56
V_G1 = 64
V_B1 = 80
V_G2 = 96
V_B2 = 112
V_FCW = 128
V_FCB = 392
V_SUBG = 480
V_LAMV = 481
V_NLI = 737
V_SUBK = 738
V_SUBE = 739
V_LNEPS = 740
V_ONE = 741
NVL = 742
DR_HBA = 0
DR_HBX = 8
DR_CLH = 16
DR_CL = 24
DR_T0 = 32
DR_NEGLAM = 48
DR_S = 50
DR_P = 64
NDER = 128


class _Op:
    __slots__ = ("eng", "fn", "reads", "writes", "dma", "deps", "signal", "tok", "idx", "epoch")


class Prog:
    def __init__(self, nc):
        self.nc = nc
        self.ops = []
        self.last_w = {}
        self.readers = {}
        self.dma_cnt = {}
        self.out_dmas = []
        self.epoch = 0

    def add(self, eng, fn, reads=(), writes=(), dma=None, final=False):
        op = _Op()
        op.eng, op.fn, op.reads, op.writes, op.dma = eng, fn, tuple(reads), tuple(writes), dma
        op.signal, op.tok = False, None
        op.epoch = self.epoch
        op.idx = len(self.ops)
        deps = {}
        for r in op.reads:
            w = self.last_w.get(r)
            if w is not None:
                deps[w.idx] = w
        for k in op.writes:
            w = self.last_w.get(k)
            if w is not None:
                deps[w.idx] = w
            for rd in self.readers.get(k, {}).values():
                deps[rd.idx] = rd
        rk = ("dma", op.idx) if dma is not None else eng
        for r in op.reads:
            self.readers.setdefault(r, {})[rk] = op
        for k in op.writes:
            self.readers[k] = {}
            self.last_w[k] = op
        latest = {}
        for i, dop in deps.items():
            if dop.dma is None:
                if dop.eng not in latest or latest[dop.eng].idx < i:
                    latest[dop.eng] = dop
        op.deps = tuple(sorted([dop for dop in deps.values() if dop.dma is not None] + list(latest.values()), key=lambda o: o.idx))
        if dma is not None:
            n = self.dma_cnt.get(dma, 0) + 1
            self.dma_cnt[dma] = n
            op.tok = (("dma", dma), 16 * n)
            if final:
                self.out_dmas.append(op)
        self.ops.append(op)
        return op

    def emit(self):
        nc = self.nc
        for op in self.ops:
            for d in op.deps:
                if d.dma is None:
                    if d.eng == "pe" and op.eng == "pe" and op.dma is None:
                        continue
                    d.signal = True
        cnt = {}
        for op in self.ops:
            if op.dma is None and op.signal:
                k = ("eng", op.eng, op.epoch)
                cnt[k] = cnt.get(k, 0) + 1
                op.tok = (k, cnt[k])
        with contextlib.ExitStack() as st:
            sems = {}
            for k in cnt:
                sems[k] = st.enter_context(nc.semaphore("s_%s_%d" % (k[1], k[2])))
            for i, k in enumerate(self.dma_cnt):
                sems[("dma", k)] = st.enter_context(nc.semaphore("d_%d" % i))
            block = st.enter_context(nc.Block())
            by_eng = {e: [o for o in self.ops if o.eng == e] for e in ENGS}
            out_dmas = self.out_dmas

            def run(eng_name, engine):
                seen = {}
                for op in by_eng[eng_name]:
                    for d in op.deps:
                        if d.tok is None:
                            continue
                        if d.dma is None and d.eng == "pe" and eng_name == "pe" and op.dma is None:
                            continue
                        k, v = d.tok
                        if seen.get(k, 0) >= v:
                            continue
                        seen[k] = v
                        engine.wait_ge(sems[k], v)
                    ins = op.fn(engine)
                    if op.dma is not None:
                        ins.then_inc(sems[op.tok[0]], 16)
                    elif op.signal:
                        ins.then_inc(sems[op.tok[0]], 1)
                if eng_name == "sp":
                    for op in out_dmas:
                        k, v = op.tok
                        if seen.get(k, 0) >= v:
                            continue
                        seen[k] = v
                        engine.wait_ge(sems[k], v)

            @block.tensor
            def _(e):
                run("pe", e)

            @block.scalar
            def _(e):
                run("act", e)

            @block.vector
            def _(e):
                run("dve", e)

            @block.gpsimd
            def _(e):
                run("pool", e)

            @block.sync
            def _(e):
                run("sp", e)


def build(nl, dbg=99, ntiles=NT):
    nc = bass.Bass("TRN2", target_bir_lowering=False)

    def dram(name, shape, kind):
        return nc.dram_tensor(name, shape, F32, kind=kind).ap()

    xT = dram("xT", [D, S], "ExternalInput")
    w_in = dram("w_in", [nl, D, INW], "ExternalInput")
    w_out = dram("w_out", [nl, D, D], "ExternalInput")
    w_up = dram("w_up", [nl, D, 2 * DFF], "ExternalInput")
    w_down = dram("w_down", [nl, DFF, D], "ExternalInput")
    bd = dram("bd", [nl, 128, 2048], "ExternalInput")
    vecs = dram("vecs", [nl, 128, NVL], "ExternalInput")
    dmat = dram("dmat", [128, 1152], "ExternalInput")
    y = dram("y", [D, S], "ExternalOutput")
    xs = [dram("xs0", [D, S], "Internal"), dram("xs1", [D, S], "Internal")] if nl > 1 else []

    with contextlib.ExitStack() as st:
        def sb(name, shape, dt):
            return st.enter_context(nc.sbuf_tensor(name, shape, dt))

        KT = sb("KT", [128, 8, S], BF16)
        VC = sb("VC", [128, 16, 1024], BF16)
        SL = [sb("SL0", [128, 8192], BF16), sb("SL1", [128, 8192], BF16)]
        XB = sb("XB", [128, 16, TT], BF16)
        R = sb("R", [128, 16, TT], F32)
        CC = sb("CC", [128, 16, TT], BF16)
        QH = sb("QH", [128, 8, TT], BF16)
        DM = sb("DM", [128, 1152], F32)
        ONES_LN = sb("ONES_LN", [128, 128], F32)
        ONES_SUB = sb("ONES_SUB", [128, 128], F32)
        ONES_B = sb("ONES_B", [128, 128], BF16)
        BD = sb("BD", [128, 2048], BF16)
        VEC = sb("VEC", [128, NVL], F32)
        DER = sb("DER", [128, NDER], F32)
        HRG = sb("HRG", [128, 8, 3], F32)
        HST = sb("HST", [128, 8], F32)
        HF = sb("HF", [128, 88, 2], F32)
        NF = 9
        F = [sb("F%d" % i, [128, 516], F32) for i in range(NF)]
        B = [sb("B%d" % i, [128, TT], BF16) for i in range(4)]
        QZ = [sb("QZ%d" % i, [128, TT], BF16) for i in range(2)]
        PS = [st.enter_context(nc.psum_tensor("PS%d" % i, [128, TT], F32)) for i in range(8)]

        P = Prog(nc)
        A = P.add
        SL16 = [s[:].rearrange("p (k n) -> p k n", k=16) for s in SL] + [CC[:]]
        SL4 = [s[:].rearrange("p (k n) -> p k n", k=4) for s in SL] + [CC[:].rearrange("p (a b) n -> p a (b n)", a=4)]
        slab_n = [0]
        slab_rot = [2]
        slab_keys = [[("SL", 0)], [("SL", 1)], [("CC", m) for m in range(16)]]

        def fk(i):
            return ("F", i)

        def vcol(c, n=1):
            return VEC[:, c:c + n]

        def load_slab16(src):
            b = slab_n[0] % slab_rot[0]
            slab_n[0] += 1
            A("pool", lambda e: e.dma_start(out=SL16[b], in_=src.rearrange("(k p) n -> p k n", p=128)),
              writes=slab_keys[b], dma=("SL", b))
            return b

        def load_slab4(src):
            b = slab_n[0] % slab_rot[0]
            slab_n[0] += 1
            A("pool", lambda e: e.dma_start(out=SL4[b], in_=src.rearrange("(k p) n -> p k n", p=128)),
              writes=slab_keys[b], dma=("SL", b))
            return b

        A("sp", lambda e: e.dma_start(out=DM[:], in_=dmat), writes=["DM"], dma="dm")
        A("dve", lambda e: e.memset(ONES_LN[:], 1.0 / D), writes=["ONES"])
        A("dve", lambda e: e.memset(ONES_SUB[:], 1.0 / 128), writes=["ONES"])
        A("dve", lambda e: e.memset(ONES_B[:], 1.0), writes=["ONES"])
        for qi in range(2):
            A("dve", lambda e, qi=qi: e.memset(QZ[qi][:], 0.0), writes=[("QZ", qi)])
        D0 = DM[:, 0:512]
        DMK = DM[:, 512:1024]

        def cbias(h, k):
            c = 1024 + h * 16 + k
            return DM[:, c:c + 1]

        evac_rr = [0]

        def evac_copy(out_ap, ps_i, wkeys):
            evac_rr[0] ^= 1
            if evac_rr[0]:
                A("act", lambda e: e.activation(out=out_ap, in_=PS[ps_i][:], func=AF.Copy),
                  reads=[("PS", ps_i)], writes=wkeys)
            else:
                A("dve", lambda e: e.tensor_copy(out=out_ap, in_=PS[ps_i][:]),
                  reads=[("PS", ps_i)], writes=wkeys)

        def proj_fm(b, j, ps_i, rhs_of_kc, rkeys, nk=16):
            for kc in range(nk):
                A("pe", lambda e, kc=kc: e.matmul(PS[ps_i][:], lhsT=SL16[b][:, kc, j * 128:(j + 1) * 128],
                                                  rhs=rhs_of_kc(kc), start=(kc == 0), stop=(kc == nk - 1)),
                  reads=slab_keys[b] + rkeys, writes=[("PS", ps_i)])

        xb_keys = [("XB", m) for m in range(16)]
        cc_keys = [("CC", m) for m in range(16)]

        def layer_norm(gcol, bcol, write_xb):
            pm, pv = 6, 7
            for m in range(16):
                A("pe", lambda e, m=m: e.matmul(PS[pm][:], lhsT=ONES_LN[:], rhs=R[:, m, :], start=(m == 0), stop=(m == 15)),
                  reads=["ONES", ("R", m)], writes=[("PS", pm)])
            for m in range(16):
                A("dve", lambda e, m=m: e.tensor_tensor(out=R[:, m, :], in0=R[:, m, :], in1=PS[pm][:], op=ALU.subtract),
                  reads=[("R", m), ("PS", pm)], writes=[("R", m)])
                sq = m % 2
                A("act", lambda e, m=m, sq=sq: e.activation(out=F[sq][:, 0:TT], in_=R[:, m, :], func=AF.Square),
                  reads=[("R", m)], writes=[fk(sq)])
                A("pe", lambda e, m=m, sq=sq: e.matmul(PS[pv][:], lhsT=ONES_LN[:], rhs=F[sq][:, 0:TT], start=(m == 0), stop=(m == 15)),
                  reads=["ONES", fk(sq)], writes=[("PS", pv)])
            A("act", lambda e: e.activation(out=F[2][:, 0:TT], in_=PS[pv][:], func=AF.Ln, bias=vcol(V_LNEPS)),
              reads=[("PS", pv), "vecs"], writes=[fk(2)])
            A("act", lambda e: e.activation(out=F[3][:, 0:TT], in_=F[2][:, 0:TT], func=AF.Exp, scale=-0.5), reads=[fk(2)], writes=[fk(3)])
            for m in range(16):
                A("dve", lambda e, m=m: e.tensor_tensor(out=R[:, m, :], in0=R[:, m, :], in1=F[3][:, 0:TT], op=ALU.mult),
                  reads=[("R", m), fk(3)], writes=[("R", m)])
                if write_xb:
                    A("act", lambda e, m=m: e.activation(out=XB[:, m, :], in_=R[:, m, :], func=AF.Identity,
                                                         scale=vcol(gcol + m), bias=vcol(bcol + m)),
                      reads=[("R", m), "vecs"], writes=[("XB", m)])
                A("act", lambda e, m=m: e.activation(out=R[:, m, :], in_=R[:, m, :], func=AF.Identity,
                                                     scale=vcol(gcol + m), bias=vcol(bcol + m)),
                  reads=[("R", m), "vecs"], writes=[("R", m)])

        for l in range(nl):
            src = xT if l == 0 else xs[(l - 1) % 2]
            srcn = "xT" if l == 0 else "xs%d" % ((l - 1) % 2)
            dst = y if l == nl - 1 else xs[l % 2]
            dstn = "y" if l == nl - 1 else "xs%d" % (l % 2)
            A("sp", lambda e, l=l: e.dma_start(out=VEC[:], in_=vecs[l]), writes=["vecs"], dma="vecs")
            A("pool", lambda e, l=l: e.dma_start(out=BD[:], in_=bd[l]), writes=["BD"], dma="BD")
            A("dve", lambda e: e.memset(HRG[:], 0.0), writes=["HRG"])
            A("dve", lambda e: e.memset(HST[:], 0.0), writes=["HST"])
            A("dve", lambda e: e.memset(HF[:], 0.0), writes=["HF"])
            A("dve", lambda e: e.tensor_scalar(out=DER[:, DR_HBA:DR_HBA + 16], in0=VEC[:, V_BA:V_BA + 16], scalar1=0.5,
                                               scalar2=None, op0=ALU.mult), reads=["vecs"], writes=["DER"])
            A("act", lambda e: e.activation(out=DER[:, DR_T0:DR_T0 + 8], in_=VEC[:, V_LAM:V_LAM + 8], func=AF.Exp, scale=-1.0),
              reads=["vecs", "DER"], writes=["DER"])
            A("act", lambda e: e.activation(out=DER[:, DR_T0:DR_T0 + 8], in_=DER[:, DR_T0:DR_T0 + 8], func=AF.Ln, bias=vcol(V_ONE)),
              reads=["vecs", "DER"], writes=["DER"])
            A("dve", lambda e: e.tensor_scalar(out=DER[:, DR_CLH:DR_CLH + 8], in0=DER[:, DR_T0:DR_T0 + 8], scalar1=-4.0,
                                               scalar2=None, op0=ALU.mult), reads=["DER"], writes=["DER"])
            A("dve", lambda e: e.tensor_scalar(out=DER[:, DR_CL:DR_CL + 8], in0=DER[:, DR_T0:DR_T0 + 8], scalar1=-8.0,
                                               scalar2=None, op0=ALU.mult), reads=["DER"], writes=["DER"])
            for i in range(2):
                A("dve", lambda e, i=i: e.tensor_tensor(out=DER[:, DR_P:DR_P + 64], in0=VEC[:, V_LAMV + 128 * i:V_LAMV + 128 * i + 64],
                                                        in1=VEC[:, V_LAMV + 128 * i + 64:V_LAMV + 128 * i + 128], op=ALU.mult),
                  reads=["vecs", "DER"], writes=["DER"])
                A("dve", lambda e, i=i: e.tensor_reduce(out=DER[:, DR_S + i:DR_S + i + 1], in_=DER[:, DR_P:DR_P + 64], axis=AX.X, op=ALU.add),
                  reads=["DER"], writes=["DER"])
            A("act", lambda e: e.activation(out=DER[:, DR_S:DR_S + 2], in_=DER[:, DR_S:DR_S + 2], func=AF.Exp),
              reads=["DER"], writes=["DER"])
            A("dve", lambda e: e.scalar_tensor_tensor(out=DER[:, DR_NEGLAM:DR_NEGLAM + 1], in0=DER[:, DR_S + 1:DR_S + 2],
                                                      scalar=vcol(V_NLI), in1=DER[:, DR_S:DR_S + 1], op0=ALU.add, op1=ALU.subtract),
              reads=["DER", "vecs"], writes=["DER"])

            for tt in range(ntiles):
                t0 = tt * TT
                P.epoch += 1
                xsrc = src[:, t0:t0 + TT].rearrange("(m p) n -> p m n", p=128)
                A("sp", lambda e, xsrc=xsrc: e.dma_start(out=R[:], in_=xsrc), reads=[("X", srcn, tt)],
                  writes=[("R", m) for m in range(16)], dma="xld")
                A("pool", lambda e, xsrc=xsrc: e.dma_start(out=XB[:], in_=xsrc), reads=[("X", srcn, tt)],
                  writes=xb_keys, dma="xbld")
                xrhs = lambda kc: XB[:, kc, :]

                for sg in range(2 if dbg >= 1 else 0):
                    b = load_slab16(w_in[l][:, 1024 + 512 * sg:1024 + 512 * (sg + 1)])
                    for j in range(4):
                        c = 4 * sg + j
                        proj_fm(b, j, c, xrhs, xb_keys)
                        A("act", lambda e, c=c: e.activation(out=QH[:, c, :], in_=PS[c][:], func=AF.Gelu_apprx_tanh),
                          reads=[("PS", c)], writes=[("QH", c)])
                fillers = []
                fstate = {}

                def mk_filler(u):
                    grp, jj = u // 4, u % 4
                    def run():
                        pi = 6 + (u % 2)
                        if jj == 0:
                            col = {0: 3072, 1: 3584, 2: 4096, 3: 4608, 4: 2048, 5: 2560}[grp]
                            fstate[grp] = load_slab16(w_in[l][:, col:col + 512])
                        b = fstate[grp]
                        if grp < 2:
                            h = 4 * grp + jj
                            proj_fm(b, jj, pi, xrhs, xb_keys)
                            evac_copy(KT[:, h, t0:t0 + TT], pi, [("KT", h, tt)])
                        elif grp < 4:
                            sv, ts = grp - 2, jj
                            for kc in range(16):
                                A("pe", lambda e, kc=kc: e.matmul(
                                    PS[pi][:], lhsT=XB[:, kc, ts * 128:(ts + 1) * 128], rhs=SL16[b][:, kc, :],
                                    start=(kc == 0), stop=(kc == 15)),
                                  reads=slab_keys[b] + [("XB", kc)], writes=[("PS", pi)])
                            evac_copy(VC[:, 4 * tt + ts, sv * 512:(sv + 1) * 512], pi, [("VC", 4 * tt + ts, sv)])
                        else:
                            h = 4 * (grp - 4) + jj
                            proj_fm(b, jj, pi, xrhs, xb_keys)
                            evac_copy(QH[:, h, :], pi, [("QH", h)])
                    return run

                if dbg >= 3:
                    fillers = [mk_filler(u) for u in range(24)]
                for sx in range(2 if dbg >= 2 else 0):
                    b = load_slab16(w_in[l][:, 512 * sx:512 * (sx + 1)])
                    for j in range(4):
                        proj_fm(b, j, j, xrhs, xb_keys)
                    for j in range(4):
                        c = 4 * sx + j
                        xh = c % 2
                        ub = c % 2
                        pa, px = 4, 5
                        A("act", lambda e, j=j, xh=xh: e.activation(out=F[xh][:, 3:515], in_=PS[j][:], func=AF.Copy),
                          reads=[("PS", j)], writes=[fk(xh)])
                        A("dve", lambda e, c=c, xh=xh: e.tensor_copy(out=F[xh][:, 0:3], in_=HRG[:, c, :]),
                          reads=["HRG"], writes=[fk(xh)])
                        A("act", lambda e, c=c, j=j: e.activation(out=F[2][:, 0:TT], in_=PS[j][:], func=AF.Identity,
                                                                  scale=vcol(V_CW + 3 * 8 + c), bias=vcol(V_CB + c)),
                          reads=[("PS", j), "vecs"], writes=[fk(2)])
                        for k in range(3):
                            A("dve", lambda e, c=c, xh=xh, k=k: e.scalar_tensor_tensor(
                                out=F[2][:, 0:TT], in0=F[xh][:, k:k + TT], scalar=vcol(V_CW + k * 8 + c), in1=F[2][:, 0:TT],
                                op0=ALU.mult, op1=ALU.add), reads=[fk(xh), fk(2), "vecs"], writes=[fk(2)])
                        A("dve", lambda e, c=c, xh=xh: e.tensor_copy(out=HRG[:, c, :], in_=F[xh][:, 512:515]),
                          reads=[fk(xh)], writes=["HRG"])
                        A("act", lambda e, ub=ub: e.activation(out=B[ub][:], in_=F[2][:, 0:TT], func=AF.Copy),
                          reads=[fk(2)], writes=[("B", ub)])
                        if fillers:
                            fillers.pop(0)()
                        A("pe", lambda e, c=c, ub=ub, pa=pa: e.matmul(PS[pa][:], lhsT=BD[:, c * 128:(c + 1) * 128], rhs=B[ub][:],
                                                                     start=True, stop=True),
                          reads=["BD", ("B", ub)], writes=[("PS", pa)])
                        A("pe", lambda e, c=c, ub=ub, px=px: e.matmul(PS[px][:], lhsT=BD[:, (8 + c) * 128:(9 + c) * 128], rhs=B[ub][:],
                                                                     start=True, stop=True),
                          reads=["BD", ("B", ub)], writes=[("PS", px)])
                        A("act", lambda e, c=c, pa=pa: e.activation(out=F[3][:, 0:TT], in_=PS[pa][:], func=AF.Tanh, scale=0.5,
                                                                    bias=DER[:, DR_HBA + c:DR_HBA + c + 1]),
                          reads=[("PS", pa), "DER"], writes=[fk(3)])
                        A("act", lambda e, c=c, px=px: e.activation(out=F[4][:, 0:TT], in_=PS[px][:], func=AF.Tanh, scale=0.5,
                                                                    bias=DER[:, DR_HBX + c:DR_HBX + c + 1]),
                          reads=[("PS", px), "DER"], writes=[fk(4)])
                        A("act", lambda e, c=c: e.activation(out=F[5][:, 0:TT], in_=F[3][:, 0:TT], func=AF.Exp,
                                                             scale=DER[:, DR_CLH + c:DR_CLH + c + 1], bias=DER[:, DR_CLH + c:DR_CLH + c + 1]),
                          reads=[fk(3), "DER"], writes=[fk(5)])
                        A("act", lambda e, c=c: e.activation(out=F[6][:, 0:TT], in_=F[3][:, 0:TT], func=AF.Exp,
                                                             scale=DER[:, DR_CL + c:DR_CL + c + 1], bias=DER[:, DR_CL + c:DR_CL + c + 1]),
                          reads=[fk(3), "DER"], writes=[fk(6)])
                        A("act", lambda e: e.activation(out=F[6][:, 0:TT], in_=F[6][:, 0:TT], func=AF.Sqrt, scale=-1.0, bias=vcol(V_ONE)),
                          reads=[fk(6), "vecs"], writes=[fk(6)])
                        A("dve", lambda e: e.scalar_tensor_tensor(out=F[7][:, 0:TT], in0=F[4][:, 0:TT], scalar=1.0, in1=F[2][:, 0:TT],
                                                                  op0=ALU.add, op1=ALU.mult), reads=[fk(4), fk(2)], writes=[fk(7)])
                        A("dve", lambda e: e.scalar_tensor_tensor(out=F[7][:, 0:TT], in0=F[7][:, 0:TT], scalar=0.5, in1=F[6][:, 0:TT],
                                                                  op0=ALU.mult, op1=ALU.mult), reads=[fk(7), fk(6)], writes=[fk(7)])
                        A("dve", lambda e, c=c: e.tensor_tensor_scan(out=F[8][:, 0:TT], data0=F[5][:, 0:TT], data1=F[7][:, 0:TT],
                                                                     initial=HST[:, c:c + 1], op0=ALU.mult, op1=ALU.add),
                          reads=[fk(5), fk(7), "HST"], writes=[fk(8)])
                        A("dve", lambda e, c=c: e.tensor_copy(out=HST[:, c:c + 1], in_=F[8][:, TT - 1:TT]),
                          reads=[fk(8)], writes=["HST"])
                        A("dve", lambda e, c=c: e.tensor_tensor(out=CC[:, c, :], in0=F[8][:, 0:TT], in1=QH[:, c, :], op=ALU.mult),
                          reads=[fk(8), ("QH", c)], writes=[("CC", c)])
                        for _ in range(2):
                            if fillers:
                                fillers.pop(0)()
                while fillers:
                    fillers.pop(0)()
                nkt = 4 * tt + 4
                steps = [(h, c, kt) for h in range(8 if dbg >= 4 else 0) for c in range(2) for kt in range(nkt)]
                LA = 3
                SBK = (0, 1, 6, 7)
                deferred = []

                def qz_copy(h, c):
                    if c == 0:
                        A("act", lambda e: e.activation(out=QZ[0][0:64, :], in_=QH[0:64, h, :], func=AF.Copy),
                          reads=[("QH", h)], writes=[("QZ", 0)])
                    else:
                        A("dve", lambda e: e.tensor_copy(out=QZ[1][64:128, :], in_=QH[64:128, h, :]),
                          reads=[("QH", h)], writes=[("QZ", 1)])

                def emit_qk(i):
                    h, c, kt = steps[i]
                    slope = 2.0 ** (-(h + 1))
                    r = kt - 4 * tt
                    n0 = 128 * r if r > 0 else 0
                    N = TT - n0
                    psi = SBK[i % 4]
                    ti = i % 2
                    bi = i % 4
                    qi = c
                    if kt == 0:
                        g = 2 * h + c
                        for gg in ([0, 1] if g == 0 else [g + 1]):
                            if gg < 16:
                                qz_copy(gg // 2, gg % 2)
                    A("pe", lambda e, h=h, kt=kt, n0=n0, N=N, psi=psi, qi=qi: e.matmul(
                        PS[psi][:, 0:N], lhsT=KT[:, h, kt * 128:(kt + 1) * 128], rhs=QZ[qi][:, n0:TT], start=True, stop=True),
                      reads=[("KT", h, kt // 4), ("QZ", qi)], writes=[("PS", psi)])
                    dsrc = DMK if r >= 0 else D0
                    A("dve", lambda e, N=N, psi=psi, ti=ti, dsrc=dsrc, slope=slope: e.scalar_tensor_tensor(
                        out=F[ti][:, 0:N], in0=dsrc[:, 0:N], scalar=8.0 * slope, in1=PS[psi][:, 0:N],
                        op0=ALU.mult, op1=ALU.add), reads=["DM", ("PS", psi)], writes=[fk(ti)])
                    if r < 0:
                        A("act", lambda e, N=N, ti=ti, bi=bi, slope=slope, r=r: e.activation(
                            out=B[bi][:, 0:N], in_=F[ti][:, 0:N], func=AF.Exp, scale=0.125, bias=float(slope * 128.0 * r)),
                          reads=[fk(ti)], writes=[("B", bi)])
                    else:
                        A("act", lambda e, N=N, ti=ti, bi=bi: e.activation(
                            out=B[bi][:, 0:N], in_=F[ti][:, 0:N], func=AF.Exp, scale=0.125),
                          reads=[fk(ti)], writes=[("B", bi)])

                def head_tail(h):
                    A("pe", lambda e: e.matmul(PS[5][:], lhsT=ONES_SUB[:], rhs=F[6][:, 0:TT], start=True, stop=True),
                      reads=["ONES", fk(6)], writes=[("PS", 5)])
                    A("act", lambda e: e.activation(out=F[7][:, 0:TT], in_=PS[5][:], func=AF.Ln, scale=vcol(V_SUBK), bias=vcol(V_SUBE)),
                      reads=[("PS", 5), "vecs"], writes=[fk(7)])
                    A("act", lambda e: e.activation(out=F[7][:, 0:TT], in_=F[7][:, 0:TT], func=AF.Exp, scale=-0.5), reads=[fk(7)], writes=[fk(7)])
                    A("dve", lambda e, h=h: e.scalar_tensor_tensor(out=CC[:, 8 + h, :], in0=F[5][:, 0:TT], scalar=vcol(V_SUBG),
                                                                  in1=F[7][:, 0:TT], op0=ALU.mult, op1=ALU.mult),
                      reads=[fk(5), fk(7), "vecs"], writes=[("CC", 8 + h)])

                def emit_pv(i):
                    h, c, kt = steps[i]
                    r = kt - 4 * tt
                    n0 = 128 * r if r > 0 else 0
                    N = TT - n0
                    bi = i % 4
                    po, pl = 2 + 2 * c, 3 + 2 * c
                    A("pe", lambda e, h=h, kt=kt, n0=n0, N=N, bi=bi, po=po: e.matmul(
                        PS[po][:, n0:TT], lhsT=VC[:, kt, h * 128:(h + 1) * 128], rhs=B[bi][:, 0:N],
                        start=(kt == 0), stop=(kt == nkt - 1)),
                      reads=[("VC", kt, h // 4), ("B", bi)], writes=[("PS", po)])
                    A("pe", lambda e, n0=n0, N=N, bi=bi, pl=pl, kt=kt: e.matmul(
                        PS[pl][:, n0:TT], lhsT=ONES_B[:], rhs=B[bi][:, 0:N],
                        start=(kt == 0), stop=(kt == nkt - 1)),
                      reads=["ONES", ("B", bi)], writes=[("PS", pl)])
                    if kt == nkt - 1:
                        A("act", lambda e, pl=pl: e.activation(out=F[2][:, 0:TT], in_=PS[pl][:], func=AF.Ln),
                          reads=[("PS", pl)], writes=[fk(2)])
                        A("act", lambda e: e.activation(out=F[2][:, 0:TT], in_=F[2][:, 0:TT], func=AF.Exp, scale=-1.0),
                          reads=[fk(2)], writes=[fk(2)])
                        A("dve", lambda e, po=po, c=c: e.tensor_tensor(out=F[3 + c][:, 0:TT], in0=PS[po][:], in1=F[2][:, 0:TT], op=ALU.mult),
                          reads=[("PS", po), fk(2)], writes=[fk(3 + c)])
                        if c == 1:
                            A("dve", lambda e: e.scalar_tensor_tensor(out=F[5][:, 0:TT], in0=F[4][:, 0:TT], scalar=DER[:, DR_NEGLAM:DR_NEGLAM + 1],
                                                                      in1=F[3][:, 0:TT], op0=ALU.mult, op1=ALU.add),
                              reads=[fk(3), fk(4), "DER"], writes=[fk(5)])
                            A("act", lambda e: e.activation(out=F[6][:, 0:TT], in_=F[5][:, 0:TT], func=AF.Square), reads=[fk(5)], writes=[fk(6)])
                            deferred.append((i + 4, h))

                for i in range(len(steps) + LA):
                    if i < len(steps):
                        emit_qk(i)
                    if i >= LA:
                        emit_pv(i - LA)
                    while deferred and deferred[0][0] <= i - LA:
                        head_tail(deferred.pop(0)[1])
                while deferred:
                    head_tail(deferred.pop(0)[1])
                for so in range(4 if dbg >= 5 else 0):
                    b = load_slab16(w_out[l][:, 512 * so:512 * (so + 1)])
                    for j in range(4):
                        m = 4 * so + j
                        pi = m % 8
                        proj_fm(b, j, pi, lambda kc: CC[:, kc, :], cc_keys)
                        A("dve", lambda e, m=m, pi=pi: e.scalar_tensor_tensor(out=R[:, m, :], in0=R[:, m, :], scalar=ALPHA, in1=PS[pi][:],
                                                                            op0=ALU.mult, op1=ALU.add),
                          reads=[("R", m), ("PS", pi)], writes=[("R", m)])
                if dbg >= 6:
                    layer_norm(V_G1, V_B1, True)
                HB = lambda hb, j: QH[:, 4 * hb + j, :]

                def down_proj(s, bdn, part=None):
                    hb = s % 2
                    for m in (range(16) if part is None else range(4 * part, 4 * part + 4)):
                        pi = 4 + (m % 4)
                        for kc in range(4):
                            A("pe", lambda e, m=m, kc=kc, pi=pi, hb=hb: e.matmul(
                                PS[pi][:], lhsT=SL4[bdn][:, kc, m * 128:(m + 1) * 128], rhs=HB(hb, kc),
                                start=(kc == 0), stop=(kc == 3)),
                              reads=slab_keys[bdn] + [("QH", 4 * hb + kc)], writes=[("PS", pi)])
                        if s == 0:
                            A("dve", lambda e, m=m, pi=pi: e.scalar_tensor_tensor(out=R[:, m, :], in0=R[:, m, :], scalar=ALPHA, in1=PS[pi][:],
                                                                                op0=ALU.mult, op1=ALU.add),
                              reads=[("R", m), ("PS", pi)], writes=[("R", m)])
                        else:
                            A("dve", lambda e, m=m, pi=pi: e.tensor_tensor(out=R[:, m, :], in0=R[:, m, :], in1=PS[pi][:], op=ALU.add),
                              reads=[("R", m), ("PS", pi)], writes=[("R", m)])

                slab_rot[0] = 3
                for s in range(11 if dbg >= 7 else 0):
                    hb = s % 2
                    for half in range(2):
                        b = load_slab16(w_up[l][:, half * DFF + 512 * s:half * DFF + 512 * (s + 1)])
                        for j in range(4):
                            proj_fm(b, j, j, xrhs, xb_keys)
                        for j in range(4):
                            ch = half * 44 + 4 * s + j
                            uh = 4 + (j % 2)
                            acc = j if half == 0 else 6 + (j % 2)
                            A("act", lambda e, j=j, uh=uh: e.activation(out=F[uh][:, 2:514], in_=PS[j][:], func=AF.Copy),
                              reads=[("PS", j)], writes=[fk(uh)])
                            A("dve", lambda e, ch=ch, uh=uh: e.tensor_copy(out=F[uh][:, 0:2], in_=HF[:, ch, :]),
                              reads=["HF"], writes=[fk(uh)])
                            A("act", lambda e, ch=ch, j=j, acc=acc: e.activation(
                                out=F[acc][:, 0:TT], in_=PS[j][:], func=AF.Identity,
                                scale=vcol(V_FCW + 2 * 88 + ch), bias=vcol(V_FCB + ch)),
                              reads=[("PS", j), "vecs"], writes=[fk(acc)])
                            for k in range(2):
                                A("dve", lambda e, ch=ch, uh=uh, acc=acc, k=k: e.scalar_tensor_tensor(
                                    out=F[acc][:, 0:TT], in0=F[uh][:, k:k + TT], scalar=vcol(V_FCW + k * 88 + ch), in1=F[acc][:, 0:TT],
                                    op0=ALU.mult, op1=ALU.add), reads=[fk(uh), fk(acc), "vecs"], writes=[fk(acc)])
                            A("dve", lambda e, ch=ch, uh=uh: e.tensor_copy(out=HF[:, ch, :], in_=F[uh][:, 512:514]),
                              reads=[fk(uh)], writes=["HF"])
                            if half == 0:
                                A("act", lambda e, acc=acc: e.activation(out=F[acc][:, 0:TT], in_=F[acc][:, 0:TT], func=AF.Gelu_apprx_tanh),
                                  reads=[fk(acc)], writes=[fk(acc)])
                            else:
                                A("dve", lambda e, j=j, acc=acc, hb=hb: e.tensor_tensor(out=HB(hb, j), in0=F[j][:, 0:TT], in1=F[acc][:, 0:TT], op=ALU.mult),
                                  reads=[fk(j), fk(acc)], writes=[("QH", 4 * hb + j)])
                                if s > 0:
                                    if j == 0:
                                        bdn_cur = load_slab4(w_down[l][512 * (s - 1):512 * s, :])
                                    down_proj(s - 1, bdn_cur, part=j)
                if dbg >= 7:
                    bdn = load_slab4(w_down[l][512 * 10:512 * 11, :])
                    down_proj(10, bdn)
                slab_rot[0] = 2
                if dbg >= 8:
                    layer_norm(V_G2, V_B2, False)
                xdst = dst[:, t0:t0 + TT].rearrange("(m p) n -> p m n", p=128)
                A("sp", lambda e, xdst=xdst: e.dma_start(out=xdst, in_=R[:]), reads=[("R", m) for m in range(16)],
                  writes=[("X", dstn, tt)], dma="xst", final=(l == nl - 1))
        P.emit()
    return nc


def _pack_layer(inp, l):
    f = lambda a: np.asarray(a, np.float32)
    v = np.zeros((128, NVL), np.float32)
    pc = lambda a, n: f(a).reshape(n, 128).T
    cw = f(inp["rg_conv_w"][l])
    for k in range(4):
        v[:, V_CW + 8 * k:V_CW + 8 * k + 8] = pc(cw[k], 8)
    v[:, V_CB:V_CB + 8] = pc(inp["rg_conv_b"][l], 8)
    v[:, V_BA:V_BA + 8] = pc(inp["rg_gate_a_b"][l], 8)
    v[:, V_BX:V_BX + 8] = pc(inp["rg_gate_x_b"][l], 8)
    v[:, V_LAM:V_LAM + 8] = pc(inp["rg_lambda"][l], 8)
    v[:, V_G1:V_G1 + 16] = pc(inp["ln_mix_g"][l], 16)
    v[:, V_B1:V_B1 + 16] = pc(inp["ln_mix_b"][l], 16)
    v[:, V_G2:V_G2 + 16] = pc(inp["ln_ffn_g"][l], 16)
    v[:, V_B2:V_B2 + 16] = pc(inp["ln_ffn_b"][l], 16)
    fw = f(inp["ffn_conv_w"][l])
    for k in range(3):
        v[:, V_FCW + 88 * k:V_FCW + 88 * (k + 1)] = pc(fw[k], 88)
    v[:, V_FCB:V_FCB + 88] = pc(inp["ffn_conv_b"][l], 88)
    v[:, V_SUBG] = f(inp["subln_g"][l])
    lamv = np.concatenate([f(inp["lam_q1"][l]), f(inp["lam_k1"][l]), f(inp["lam_q2"][l]), f(inp["lam_k2"][l])])
    v[:, V_LAMV:V_LAMV + 256] = lamv[None, :]
    lam_init = 0.8 - 0.6 * math.exp(-0.3 * l)
    k2 = 1.0 / (1.0 - lam_init) ** 2
    v[:, V_NLI] = -lam_init
    v[:, V_SUBK] = k2
    v[:, V_SUBE] = 1e-5 * k2
    v[:, V_LNEPS] = 1e-5
    v[:, V_ONE] = 1.0
    bdl = np.zeros((128, 2, 8, 128), np.float32)
    for g, name in enumerate(("rg_gate_a_w", "rg_gate_x_w")):
        w = f(inp[name][l])
        for c in range(8):
            bdl[0:64, g, c, 0:64] = w[2 * c]
            bdl[64:128, g, c, 64:128] = w[2 * c + 1]
    return v, bdl.reshape(128, 2048)


def _consts():
    dm = np.zeros((128, 1152), np.float32)
    p = np.arange(128, dtype=np.float32)[:, None]
    j = np.arange(512, dtype=np.float32)[None, :]
    dm[:, 0:512] = p - j
    dm[:, 512:1024] = np.where(p - j <= 0, p - j, NEG)
    for h in range(8):
        for k in range(16):
            dm[:, 1024 + 16 * h + k] = -(2.0 ** (-(h + 1))) * 128.0 * k
    return dm


FUSED = True
_NC_CACHE = {}


def _get_nc(nl):
    if nl not in _NC_CACHE:
        _NC_CACHE[nl] = build(nl)
    return _NC_CACHE[nl]


def kernel(**inputs):
    inp = {k: np.asarray(v) for k, v in inputs.items()}
    x = np.asarray(inp["x"], np.float32)
    nb = x.shape[0]
    dm = _consts()
    packs = [_pack_layer(inp, l) for l in range(DEPTH)]
    xTs = [np.ascontiguousarray(x[b].T) for b in range(nb)]
    f = lambda a: np.ascontiguousarray(np.asarray(a, np.float32))
    if FUSED:
        nc = _get_nc(DEPTH)
        vec_all = np.stack([p[0] for p in packs])
        bd_all = np.stack([p[1] for p in packs])
        shared = {"w_in": f(inp["w_in"]), "w_out": f(inp["w_out"]), "w_up": f(inp["w_up"]), "w_down": f(inp["w_down"]),
                  "bd": bd_all, "vecs": vec_all, "dmat": dm}
        in_maps = [dict(shared, xT=xTs[b]) for b in range(nb)]
        res = run_bass_kernel_spmd(nc, in_maps, core_ids=list(range(nb)))
        outs = [res.results[b]["y"] for b in range(nb)]
    else:
        nc = _get_nc(1)
        cur = xTs
        for l in range(DEPTH):
            shared = {"w_in": f(inp["w_in"][l:l + 1]), "w_out": f(inp["w_out"][l:l + 1]), "w_up": f(inp["w_up"][l:l + 1]),
                      "w_down": f(inp["w_down"][l:l + 1]), "bd": packs[l][1][None], "vecs": packs[l][0][None], "dmat": dm}
            in_maps = [dict(shared, xT=cur[b]) for b in range(nb)]
            res = run_bass_kernel_spmd(nc, in_maps, core_ids=list(range(nb)))
            cur = [np.ascontiguousarray(res.results[b]["y"]) for b in range(nb)]
        outs = cur
    return np.stack([np.ascontiguousarray(o.T) for o in outs]).astype(np.float32)
```
